# Optimizing a Trainium2 kernel written in Bass

```python
import math
import jax, jax.numpy as jnp
from jax import lax
import numpy as np

D_MODEL = 1024
BATCH = 16
SEQ = 4096
DEPTH = 1

HEAD_DIM = 64
MIX_WIDTH = D_MODEL
A_HEADS = (MIX_WIDTH // 2) // HEAD_DIM
A_KV_HEADS = 2
A_GROUP = A_HEADS // A_KV_HEADS
B_HEADS = (MIX_WIDTH // 2) // HEAD_DIM
WINDOW = 128
BLOCK = 128
D_FF = 4 * D_MODEL
EPS = 1e-6

A_Q_W = A_HEADS * HEAD_DIM
A_KV_W = A_KV_HEADS * HEAD_DIM
B_W = B_HEADS * HEAD_DIM
IN_SPLITS = tuple(np.cumsum([A_Q_W, A_KV_W, A_KV_W, B_W, B_W, B_W]).tolist())
IN_WIDTH = A_Q_W + 2 * A_KV_W + 3 * B_W + B_HEADS

kernel_name = "hybrid_swa_sinks_fox_sqrelu"


def rmsnorm(x, g):
    x32 = x.astype(jnp.float32)
    y = x32 * lax.rsqrt(jnp.mean(x32 * x32, axis=-1, keepdims=True) + EPS)
    return (y * g.astype(jnp.float32)).astype(x.dtype)


def alibi_slopes(n):
    return jnp.exp2(-(8.0 / n) * (jnp.arange(n, dtype=jnp.float32) + 1.0))


def swa_sinks_attention(q, k, v, sinks):
    b, s, _, d = q.shape
    nb = s // BLOCK
    scale = 1.0 / math.sqrt(d)
    qb = q.reshape(b, nb, BLOCK, A_KV_HEADS, A_GROUP, d)
    pad = ((0, 0), (BLOCK, 0), (0, 0), (0, 0))
    kp = jnp.pad(k, pad).reshape(b, nb + 1, BLOCK, A_KV_HEADS, d)
    vp = jnp.pad(v, pad).reshape(b, nb + 1, BLOCK, A_KV_HEADS, d)
    kb = jnp.concatenate([kp[:, :-1], kp[:, 1:]], axis=2)
    vb = jnp.concatenate([vp[:, :-1], vp[:, 1:]], axis=2)
    scores = jnp.einsum('bnqkgd,bnskd->bnkgqs', qb, kb).astype(jnp.float32) * scale
    qpos = BLOCK + jnp.arange(BLOCK)
    kpos = jnp.arange(2 * BLOCK)
    dist = qpos[:, None] - kpos[None, :]
    band = (dist >= 0) & (dist < WINDOW)
    first_pad = (jnp.arange(nb) == 0)[:, None, None] & (kpos < BLOCK)[None, None, :]
    valid = band[None] & ~first_pad
    slopes = alibi_slopes(A_HEADS).reshape(A_KV_HEADS, A_GROUP)
    alibi = -slopes[:, :, None, None] * dist.astype(jnp.float32)[None, None]
    scores = scores + alibi[None, None]
    scores = jnp.where(valid[None, :, None, None], scores, -jnp.inf)
    sink = sinks.astype(jnp.float32).reshape(1, 1, A_KV_HEADS, A_GROUP, 1, 1)
    m = jnp.maximum(jnp.max(scores, axis=-1, keepdims=True), sink)
    p = jnp.exp(scores - m)
    denom = jnp.sum(p, axis=-1, keepdims=True) + jnp.exp(sink - m)
    p = (p / denom).astype(v.dtype)
    out = jnp.einsum('bnkgqs,bnskd->bnqkgd', p, vb)
    return out.reshape(b, s, A_HEADS, d)


def forgetting_attention(q, k, v, log_f):
    b, s, h, d = q.shape
    nb = s // BLOCK
    scale = 1.0 / math.sqrt(d)
    c = jnp.cumsum(log_f, axis=1)
    c_keys = jnp.transpose(c, (0, 2, 1))
    qb = jnp.moveaxis(q.reshape(b, nb, BLOCK, h, d), 1, 0)
    cb = jnp.moveaxis(c.reshape(b, nb, BLOCK, h), 1, 0)
    kpos = jnp.arange(s)

    def block_step(args):
        qi, ci, i = args
        sc = jnp.einsum('bqhd,bshd->bhqs', qi, k).astype(jnp.float32) * scale
        bias = jnp.transpose(ci, (0, 2, 1))[..., None] - c_keys[:, :, None, :]
        qpos = i * BLOCK + jnp.arange(BLOCK)
        causal = kpos[None, :] <= qpos[:, None]
        sc = jnp.where(causal[None, None], sc + bias, -jnp.inf)
        p = jax.nn.softmax(sc, axis=-1).astype(v.dtype)
        return jnp.einsum('bhqs,bshd->bqhd', p, v)

    out = lax.map(block_step, (qb, cb, jnp.arange(nb)))
    return jnp.moveaxis(out, 0, 1).reshape(b, s, h, d)


def setup_inputs(seed: int = 0) -> dict:
    key = jax.random.key(seed)
    ks = jax.random.split(key, 14)
    f32 = jnp.float32
    x = jax.random.normal(ks[0], (BATCH, SEQ, D_MODEL), f32)
    attn_norm_g = 1.0 + 0.02 * jax.random.normal(ks[1], (D_MODEL,), f32)
    w_in = jax.random.normal(ks[2], (D_MODEL, IN_WIDTH), f32) * D_MODEL ** -0.5
    b_forget = 2.0 + 0.5 * jax.random.normal(ks[3], (B_HEADS,), f32)
    q_norm_a = 1.0 + 0.02 * jax.random.normal(ks[4], (HEAD_DIM,), f32)
    k_norm_a = 1.0 + 0.02 * jax.random.normal(ks[5], (HEAD_DIM,), f32)
    sink_logits = 0.5 * jax.random.normal(ks[6], (A_HEADS,), f32)
    q_norm_b = 1.0 + 0.02 * jax.random.normal(ks[7], (HEAD_DIM,), f32)
    k_norm_b = 1.0 + 0.02 * jax.random.normal(ks[8], (HEAD_DIM,), f32)
    w_out = jax.random.normal(ks[9], (MIX_WIDTH, D_MODEL), f32) * MIX_WIDTH ** -0.5
    mlp_norm_g = 1.0 + 0.02 * jax.random.normal(ks[10], (D_MODEL,), f32)
    w_up = jax.random.normal(ks[11], (D_MODEL, D_FF), f32) * D_MODEL ** -0.5
    w_down = jax.random.normal(ks[12], (D_FF, D_MODEL), f32) * D_FF ** -0.5
    return {"x": x, "attn_norm_g": attn_norm_g, "w_in": w_in, "b_forget": b_forget,
            "q_norm_a": q_norm_a, "k_norm_a": k_norm_a, "sink_logits": sink_logits,
            "q_norm_b": q_norm_b, "k_norm_b": k_norm_b, "w_out": w_out,
            "mlp_norm_g": mlp_norm_g, "w_up": w_up, "w_down": w_down}


def reference(x, attn_norm_g, w_in, b_forget, q_norm_a, k_norm_a, sink_logits,
              q_norm_b, k_norm_b, w_out, mlp_norm_g, w_up, w_down):
    b, s, _ = x.shape
    for _layer in range(DEPTH):
        xn = rmsnorm(x, attn_norm_g)
        proj = jnp.einsum('bsd,de->bse', xn, w_in)
        qa, ka, va, qb, kb, vb, f_logit = jnp.split(proj, IN_SPLITS, axis=-1)
        qa = rmsnorm(qa.reshape(b, s, A_HEADS, HEAD_DIM), q_norm_a)
        ka = rmsnorm(ka.reshape(b, s, A_KV_HEADS, HEAD_DIM), k_norm_a)
        va = va.reshape(b, s, A_KV_HEADS, HEAD_DIM)
        out_a = swa_sinks_attention(qa, ka, va, sink_logits)
        qb = rmsnorm(qb.reshape(b, s, B_HEADS, HEAD_DIM), q_norm_b)
        kb = rmsnorm(kb.reshape(b, s, B_HEADS, HEAD_DIM), k_norm_b)
        vb = vb.reshape(b, s, B_HEADS, HEAD_DIM)
        log_f = jax.nn.log_sigmoid(f_logit.astype(jnp.float32) + b_forget.astype(jnp.float32))
        out_b = forgetting_attention(qb, kb, vb, log_f)
        mixed = jnp.concatenate([out_a.reshape(b, s, A_Q_W), out_b.reshape(b, s, B_W)], axis=-1)
        x = x + jnp.einsum('bse,ed->bsd', mixed, w_out)
        hn = rmsnorm(x, mlp_norm_g)
        hid = jnp.square(jax.nn.relu(jnp.einsum('bsd,df->bsf', hn, w_up)))
        x = x + jnp.einsum('bsf,fd->bsd', hid, w_down)
    return x
```

```python
import numpy as np
import concourse.bass as bass
import concourse.mybir as mybir
from concourse.bass_utils import run_bass_kernel_spmd

F32 = mybir.dt.float32
BF16 = mybir.dt.bfloat16
ALU = mybir.AluOpType
AF = mybir.ActivationFunctionType
AX = mybir.AxisListType

SEM_ROLL = 30000
D = 1024
KC = 8
HD = 64
NHA = 8
NKV = 2
NHB = 8
INW = 2312
EPS = 1e-6
NEG = -30000.0
NSLOT = 3


class DmaSem:
    def __init__(self, nc, name):
        self.nc = nc
        self.name = name
        self.sem = nc.alloc_semaphore(name)
        self.count = 0


class Sched:
    def __init__(self, nc):
        self.nc = nc
        self.eng = {"pe": nc.tensor, "act": nc.scalar, "dve": nc.vector, "pool": nc.gpsimd, "sp": nc.sync}
        self.sem = {}
        self.cnt = {}
        self.nsem = 0
        for e in ("pe", "act", "dve", "pool"):
            self._new_sem(e)
        self.waited = {e: {} for e in self.eng}
        self.res = {}
        self.n_ops = 0
        self.sh_owner = {}

    def _expand(self, reads, writes):
        r2, w2 = [], []
        for lst, is_w in ((reads, False), (writes, True)):
            for name in lst:
                if "|" in name:
                    fam, fine, gr = name.split("|")
                    (w2 if is_w else r2).append(fine)
                    for g in gr.split(","):
                        if self.sh_owner.get(g) != fam:
                            self.sh_owner[g] = fam
                            w2.append(g)
                        else:
                            r2.append(g)
                else:
                    (w2 if is_w else r2).append(name)
        return r2, w2

    def _new_sem(self, e):
        self.sem[e] = self.nc.alloc_semaphore(f"s_{e}_{self.nsem}")
        self.nsem += 1
        self.cnt[e] = 0

    def _wait(self, e, tok):
        if tok is None:
            return
        sem, val, src = tok
        if src == e and e == "pe":
            return
        w = self.waited[e]
        if w.get(sem.name, 0) >= val:
            return
        w[sem.name] = val
        self.eng[e].wait_ge(sem, val)

    @staticmethod
    def _is_psum(r):
        return r.startswith("mm") or r.startswith("tp") or r == "misc"

    def deps(self, e, reads, writes):
        for r in reads:
            st = self.res.get(r)
            if st:
                self._wait(e, st["w"])
                if self._is_psum(r):
                    for t in st["r"].values():
                        if t[2] != e:
                            self._wait(e, t)
        for r in writes:
            st = self.res.get(r)
            if st:
                self._wait(e, st["w"])
                for t in st["r"].values():
                    self._wait(e, t)

    def commit(self, tok, reads, writes):
        for r in reads:
            st = self.res.setdefault(r, {"w": None, "r": {}})
            st["r"][tok[0].name] = tok
        for r in writes:
            self.res[r] = {"w": tok, "r": {}}

    def _signal(self, e, ins, reads, writes):
        if self.cnt[e] >= SEM_ROLL:
            self._new_sem(e)
        self.cnt[e] += 1
        ins.then_inc(self.sem[e], 1)
        tok = (self.sem[e], self.cnt[e], e)
        self.commit(tok, reads, writes)
        self.n_ops += 1
        return tok

    def op(self, e, fn, reads=(), writes=()):
        reads, writes = self._expand(reads, writes)
        self.deps(e, reads, writes)
        return self._signal(e, fn(self.eng[e]), reads, writes)

    def group(self, e, fns, reads=(), writes=()):
        reads, writes = self._expand(reads, writes)
        self.deps(e, reads, writes)
        ins = None
        for fn in fns:
            ins = fn(self.eng[e])
        return self._signal(e, ins, reads, writes)

    def dma(self, q, dsem, out, in_, reads=(), writes=(), **kw):
        reads, writes = self._expand(reads, writes)
        self.deps(q, reads, writes)
        ins = self.eng[q].dma_start(out=out, in_=in_, **kw)
        if dsem.count >= SEM_ROLL:
            dsem.sem = self.nc.alloc_semaphore(f"{dsem.name}_r{self.nsem}")
            self.nsem += 1
            dsem.count = 0
        dsem.count += 16
        ins.then_inc(dsem.sem, 16)
        tok = (dsem.sem, dsem.count, "dma")
        self.commit(tok, reads, writes)
        return tok

    def wait_all(self, e):
        best = {}
        for st in self.res.values():
            for t in [st["w"], *st["r"].values()]:
                if t is not None and (t[0].name not in best or best[t[0].name][1] < t[1]):
                    best[t[0].name] = t
        for t in best.values():
            self._wait(e, t)


def alibi_slopes():
    return [2.0 ** (-(h + 1)) for h in range(NHA)]


def host_consts():
    p = np.arange(128, dtype=np.float32)
    c = {}
    c["ident"] = np.eye(128, dtype=np.float32)
    c["tri"] = (p[:, None] <= p[None, :]).astype(np.float32)
    sel = np.zeros((128, 128), np.float32)
    sel[127, :] = 1.0
    c["sel"] = sel
    mc = np.where(p[:, None] <= p[None, :], 0.0, NEG).astype(np.float32)
    c["maskC"] = np.tile(mc, (1, 4))
    sl = alibi_slopes()
    mp = np.zeros((NKV, 128, 512), np.float32)
    for g in range(NKV):
        for hh in range(4):
            s_ = sl[g * 4 + hh]
            mp[g, :, hh * 128:(hh + 1) * 128] = np.where(p[:, None] > p[None, :], -128.0 * s_, NEG)
    c["maskP"] = mp
    aq = np.zeros((128, NHA, 2), np.float32)
    for h in range(NHA):
        aq[:, h, 0] = -sl[h] * p
        aq[:, h, 1] = sl[h]
    c["augqA"] = aq
    ak = np.zeros((128, 2), np.float32)
    ak[:, 0] = 1.0
    ak[:, 1] = p
    c["augkA"] = ak
    return c


def build_program(NSEQ, S, DFF):
    NB = S // 128
    NCH = S // 512
    NG = DFF // 512
    NTOK = NSEQ * S
    nc = bass.Bass("TRN2", target_bir_lowering=False)
    S_ = Sched(nc)

    def dram_in(name, shape, dt=F32):
        return nc.dram_tensor(name, shape, dt, kind="ExternalInput").ap()

    x = dram_in("x", [NTOK, D])
    w_in = dram_in("w_in", [D, INW])
    w_out = dram_in("w_out", [D, D])
    w_up = dram_in("w_up", [D, DFF])
    w_down = dram_in("w_down", [DFF, D])
    gA_d = dram_in("gA", [128, D])
    gM_d = dram_in("gM", [128, D])
    bfg_d = dram_in("bfg", [128, NHB])
    gains_d = dram_in("gains", [64, 4])
    sinks_d = dram_in("sinks", [128, NHA])
    ident_d = dram_in("ident", [128, 128])
    tri_d = dram_in("tri", [128, 128])
    sel_d = dram_in("sel", [128, 128])
    maskC_d = dram_in("maskC", [128, 512])
    maskP_d = dram_in("maskP", [NKV, 128, 512])
    augqA_d = dram_in("augqA", [128, NHA, 2])
    augkA_d = dram_in("augkA", [128, 2])
    y = nc.dram_tensor("y", [NTOK, D], F32, kind="ExternalOutput").ap()

    NSLAB = 5 + 2 + 2 * NG
    wscr = nc.dram_tensor("wscr", [NSLAB, 128, 4096], BF16).ap()
    SL_I = [0, 1, 2, 3, 4]
    SL_O = [5, 6]
    SL_U = [7 + 2 * g for g in range(NG)]
    SL_D = [8 + 2 * g for g in range(NG)]

    def sb(name, shape, dt):
        return nc.alloc_sbuf_tensor("sb_" + name, shape, dt)

    wslot = [sb(f"wslot{k}", [128, 4096], BF16) for k in range(NSLOT)]
    KT_B = sb("KT_B", [128, NHB, S], BF16)
    V_B = sb("V_B", [128, NB, NHB * 65], BF16)
    KT_A = sb("KT_A", [128, NKV, 8 * 128], BF16)
    V_A = sb("V_A", [128, 8, NKV * 65], BF16)
    xh = sb("xh", [128, 4, D], F32)
    xnT = sb("xnT", [128, KC, 512], BF16)
    QT_B = sb("QT_B", [128, NHB, 512], BF16)
    QT_A = sb("QT_A", [128, NKV, 4, 4 * 128], BF16)
    mixT = sb("mixT", [128, KC, 512], BF16)
    gbuf = sb("gbuf", [128, D], F32)
    ident_bf = sb("ident_bf", [128, 128], BF16)
    tri = sb("tri", [128, 128], F32)
    sel = sb("sel", [128, 128], F32)
    maskC = sb("maskC", [128, 512], BF16)
    maskP = sb("maskP", [128, NKV, 512], BF16)
    ones_f = sb("ones_f", [128, 128], F32)
    negc = sb("negc", [128, NB, NHB], F32)
    cchunk = sb("cchunk", [128, 4, NHB], F32)
    bfg = sb("bfg", [128, NHB], F32)
    gains = sb("gains", [64, 4], F32)
    scaleA = sb("scaleA", [128, 1], F32)
    scaleB = sb("scaleB", [128, 1], F32)
    sinkb = sb("sinkb", [128, NHA], F32)
    small = sb("small", [128, 192], F32)
    cprev = sb("cprev", [128, NHB], F32)
    shared = sb("shared", [128, 19456], mybir.dt.uint8)

    def view(off, shape, dt):
        nbytes = int(np.prod(shape[1:])) * (4 if dt == F32 else 2)
        ap = shared[:, off:off + nbytes].bitcast(dt)
        if len(shape) == 3:
            ap = ap.rearrange("p (a b) -> p a b", a=shape[1])
        return ap

    xpre = view(0, [128, 1024], F32)
    R_xpre = ["A|xpre|sh0,sh1"]
    stage = [view(0, [128, 512], F32), view(2048, [128, 512], F32)]
    sqb = [view(4096, [128, 512], F32), view(6144, [128, 512], F32)]
    junk = view(4096, [128, 1024], F32)
    xnb = [view(8192, [128, 1024], BF16), view(10240, [128, 1024], BF16)]
    qAt = [view(12288, [128, NHA, 66], BF16), view(12288 + 1056, [128, NHA, 66], BF16)]
    qBt = [view(14400, [128, NHB, 68], BF16), view(14400 + 1088, [128, NHB, 68], BF16)]
    kBt = [view(16576, [128, NHB, 68], BF16), view(16576 + 1088, [128, NHB, 68], BF16)]
    kAt = [view(18752, [128, NKV, 66], BF16), view(18752 + 264, [128, NKV, 66], BF16)]
    PT = [view(1024 * k, [128, 512], BF16) for k in range(6)]
    PTp = [view(2048 * k, [128, 1024], BF16) for k in range(3)]
    rden = view(8192, [128, 512], F32)
    bc_sb = [view(6144, [128, 512], F32)]
    mtok = [view(10240, [128, 256], BF16), view(10240 + 512, [128, 256], BF16)]
    yout = [view(off, [128, 512], F32) for off in (0, 2048, 8192, 10240)]
    R_yout = [["Y|yout0|sh0"], ["Y|yout1|sh1"], ["Y|yout2|sh4"], ["Y|yout3|sh5"]]
    hid = [view(0, [128, 4, 512], BF16), view(4096, [128, 4, 512], BF16)]
    rtmp = [view(8192, [128, 512], F32), view(10240, [128, 512], F32)]
    R_stage = [["A|stage0|sh0"], ["A|stage1|sh1"]]
    R_sqb = [["A|sqb0|sh2"], ["A|sqb1|sh3"]]
    R_junk = R_sqb[0] + R_sqb[1]
    R_xnb = [["A|xnb0|sh4"], ["A|xnb1|sh5"]]
    R_PT = [[f"BC|PT{k}|sh{k // 2}"] for k in range(6)]
    R_bc = [["BC|bc0|sh3"]]
    R_mtok = [["BC|mtok0|sh5"], ["BC|mtok1|sh5"]]
    R_hid = [["E|hid0|sh0,sh1"], ["E|hid1|sh2,sh3"]]
    R_rtmp = [["E|rtmp0|sh4"], ["E|rtmp1|sh5"]]

    pairb = [nc.alloc_psum_tensor(f"pair{k}", [128, 1024], F32) for k in range(2)]
    mm4 = nc.alloc_psum_tensor("mm4", [128, 512], F32)
    mmb = [pairb[0][:, 0:512], pairb[0][:, 512:1024], pairb[1][:, 0:512], pairb[1][:, 512:1024], mm4[:]]
    tpb = [nc.alloc_psum_tensor(f"tp{k}", [128, 1024], BF16) for k in range(2)]
    misc = nc.alloc_psum_tensor("misc", [128, 512], F32)
    tp1_f32 = tpb[1][:].bitcast(F32)
    mm_i = [0]

    def next_mm():
        k = mm_i[0] % 5
        mm_i[0] += 1
        return k

    tp_i = [0]

    def next_tp():
        k = tp_i[0] % 2
        tp_i[0] += 1
        return k

    d_const = DmaSem(nc, "d_const")
    d_constp = DmaSem(nc, "d_constp")
    d_scr = [DmaSem(nc, f"d_scr{k}") for k in range(NSLAB)]
    d_slot = [DmaSem(nc, f"d_slot{k}") for k in range(NSLOT)]
    d_x = [DmaSem(nc, f"d_x{k}") for k in range(4)]
    d_xp = [DmaSem(nc, f"d_xp{k}") for k in range(4)]
    d_yb = [DmaSem(nc, f"d_yb{k}") for k in range(4)]
    d_y = [DmaSem(nc, f"d_y{k}") for k in range(4)]
    d_g = DmaSem(nc, "d_g")

    cst_tmp = view(0, [128, 512], F32)
    cst_tmp2 = view(2048, [128, 1024], F32)

    def load_const(dst, src, name):
        S_.dma("sp", d_const, dst, src, writes=[name])

    load_const(tri[:], tri_d[:], "tri")
    load_const(sel[:], sel_d[:], "sel")
    load_const(bfg[:], bfg_d[:], "bfg")
    load_const(gains[:], gains_d[:], "gains")
    load_const(sinkb[:], sinks_d[:], "sinkb")
    load_const(gbuf[:], gA_d[:], "gbuf")
    S_.dma("pool", d_constp, ident_bf[:], ident_d[:], writes=["ident_bf"])
    S_.dma("pool", d_constp, maskC[:], maskC_d[:], writes=["maskC"])
    S_.dma("pool", d_constp, maskP[:], maskP_d.rearrange("g p n -> p g n"), writes=["maskP"])
    for k in range(2):
        S_.dma("pool", d_constp, qAt[k][:, :, 64:66], augqA_d[:], writes=[f"qAt{k}"])
        S_.dma("pool", d_constp, kAt[k][:, :, 64:66], augkA_d.unsqueeze(1).to_broadcast([128, NKV, 2]), writes=[f"kAt{k}"])
    tok_const = (d_const.sem, d_const.count, "dma")
    for r in ["tri", "sel", "bfg", "gains", "sinkb", "gbuf"]:
        S_.res[r] = {"w": tok_const, "r": {}}
    tok_constp = (d_constp.sem, d_constp.count, "dma")
    for r in ["ident_bf", "maskC", "maskP", "qAt0", "qAt1", "kAt0", "kAt1"]:
        S_.res[r] = {"w": tok_constp, "r": {}}

    def scr3(k, a):
        return wscr[k].rearrange("p (a b) -> p a b", a=a)

    w_in_v = w_in.rearrange("(kc p) n -> p kc n", p=128)
    w_out_v = w_out.rearrange("(kc p) n -> p kc n", p=128)
    w_up_v = w_up.rearrange("(kc p) n -> p kc n", p=128)
    w_down_v = w_down.rearrange("(fc p) n -> p fc n", p=128)

    def conv(k, dst, src):
        S_.dma("pool", d_scr[k], dst, src, writes=[f"scr{k}"])

    conv(4, scr3(4, KC)[:, :, 0:256], w_in_v[:, :, 512:768])
    conv(4, scr3(4, KC)[:, :, 256:264], w_in_v[:, :, 2304:2312])
    S_.res["scr4"]["w"] = (d_scr[4].sem, d_scr[4].count, "dma")
    conv(0, scr3(0, KC), w_in_v[:, :, 0:512])
    conv(2, scr3(2, KC), w_in_v[:, :, 1280:1792])
    conv(1, scr3(1, KC), w_in_v[:, :, 768:1280])
    conv(3, scr3(3, KC), w_in_v[:, :, 1792:2304])
    for dh in range(2):
        conv(SL_O[dh], scr3(SL_O[dh], KC), w_out_v[:, :, dh * 512:(dh + 1) * 512])
    for g in range(NG):
        conv(SL_U[g], scr3(SL_U[g], KC), w_up_v[:, :, g * 512:(g + 1) * 512])
        conv(SL_D[g], scr3(SL_D[g], 4), w_down_v[:, g * 4:(g + 1) * 4, :])

    S_.op("pool", lambda e: e.memset(ones_f[:], 1.0), writes=["ones_f"])
    S_.op("pool", lambda e: e.memset(scaleA[:], 1.0), writes=["scaleA"])
    S_.op("pool", lambda e: e.memset(scaleB[:], 1.0), writes=["scaleB"])
    S_.op("pool", lambda e: e.memset(V_B[:].rearrange("p n (h c) -> p (n h) c", c=65)[:, :, 64:65], 1.0), writes=["V_B_ones"])
    S_.op("pool", lambda e: e.memset(V_A[:].rearrange("p n (h c) -> p (n h) c", c=65)[:, :, 64:65], 1.0), writes=["V_A_ones"])
    for k in range(2):
        S_.op("pool", lambda e, k=k: e.memset(kBt[k][:, :, 64:65], 1.0), writes=[f"kBt{k}"])
        S_.op("pool", lambda e, k=k: e.memset(qBt[k][:, :, 65:68], 1.0), writes=[f"qBt{k}"])
    S_.op("dve", lambda e: e.scalar_tensor_tensor(out=scaleA[0:64, :], in0=gains[:, 0:1], scalar=0.125, in1=gains[:, 1:2],
                                                   op0=ALU.mult, op1=ALU.mult), reads=["gains"], writes=["scaleA"])
    S_.op("dve", lambda e: e.scalar_tensor_tensor(out=scaleB[0:64, :], in0=gains[:, 2:3], scalar=0.125, in1=gains[:, 3:4],
                                                   op0=ALU.mult, op1=ALU.mult), reads=["gains"], writes=["scaleB"])
    S_.op("act", lambda e: e.activation(out=sinkb[:], in_=sinkb[:], func=AF.Exp), reads=["sinkb"], writes=["sinkb"])

    wseq = []
    for _q in range(NSEQ):
        for _c in range(NCH):
            wseq += [SL_I[4], SL_I[0], SL_I[2], SL_I[1], SL_I[3]]
            wseq += SL_O
            order = []
            for g in range(NG + 1):
                if g < NG:
                    order.append(SL_U[g])
                if g >= 1:
                    order.append(SL_D[g - 1])
            wseq += order
    ws = {"pos": 0, "loaded": 0}

    def ws_load(i):
        s = i % NSLOT
        if wseq[i] == SL_I[4]:
            S_.dma("sp", d_slot[s], wslot[s][:].rearrange("p (a b) -> p a b", a=KC)[:, :, 0:264], scr3(wseq[i], KC)[:, :, 0:264],
                   reads=[f"scr{wseq[i]}"], writes=[f"wslot{s}"])
        else:
            S_.dma("sp", d_slot[s], wslot[s][:], wscr[wseq[i]], reads=[f"scr{wseq[i]}"], writes=[f"wslot{s}"])

    def ws_get(expect, ahead=0):
        i = ws["pos"] + ahead
        assert wseq[i] == expect, (i, wseq[i], expect)
        while ws["loaded"] <= i:
            ws_load(ws["loaded"])
            ws["loaded"] += 1
        return i % NSLOT

    def ws_done():
        ws["pos"] += 1
        nxt = ws["pos"] - 1 + NSLOT
        if nxt < len(wseq) and ws["loaded"] == nxt:
            ws_load(nxt)
            ws["loaded"] += 1

    for i in range(min(NSLOT, len(wseq))):
        ws_load(i)
        ws["loaded"] += 1

    def run_pipe(items, nst):
        n = len(items)
        for step in range(n + nst - 1):
            for s in reversed(range(nst)):
                t = step - s
                if 0 <= t < n and items[t] is not None and s < len(items[t]) and items[t][s] is not None:
                    items[t][s]()

    def norm_item(i, gname, evac="dve", pre=False):
        xi = xpre[:] if pre else xh[:, i, :]
        xres = R_xpre if pre else [f"xh{i}"]
        b = i % 2
        ssc = small[:, 4 + i:5 + i]
        rsc = small[:, i:i + 1]
        st = {}

        def s0():
            S_.op("act", lambda e: e.activation(out=junk[:], in_=xi, func=AF.Square, accum_out=ssc),
                  reads=xres, writes=R_junk + [f"ss{i}"])
            if pre:
                S_.op("dve", lambda e: e.tensor_copy(xh[:, i, :], xi), reads=xres, writes=[f"xh{i}"])

        def s1():
            S_.op("dve", lambda e: e.tensor_scalar(out=rsc, in0=ssc, scalar1=1.0 / D, scalar2=EPS, op0=ALU.mult, op1=ALU.add),
                  reads=[f"ss{i}"], writes=[f"rs{i}"])

        def s2():
            S_.op("act", lambda e: e.activation(out=rsc, in_=rsc, func=AF.Ln), reads=[f"rs{i}"], writes=[f"rs{i}"])
            S_.op("act", lambda e: e.activation(out=rsc, in_=rsc, func=AF.Exp, scale=-0.5), reads=[f"rs{i}"], writes=[f"rs{i}"])

        def s3():
            S_.op("dve", lambda e: e.scalar_tensor_tensor(out=xnb[b][:], in0=xi, scalar=rsc, in1=gbuf[:], op0=ALU.mult, op1=ALU.mult),
                  reads=xres + [f"rs{i}", gname], writes=R_xnb[b])

        def s4():
            t = next_tp()
            st["t"] = t
            S_.group("pe", [lambda e, kc=kc: e.transpose(tpb[t][:, kc * 128:(kc + 1) * 128], xnb[b][:, kc * 128:(kc + 1) * 128], ident_bf[:])
                            for kc in range(KC)], reads=R_xnb[b] + ["ident_bf"], writes=[f"tp{t}"])

        def s5():
            t = st["t"]
            if evac == "act":
                S_.op("act", lambda e: e.activation(out=xnT[:, :, i * 128:(i + 1) * 128], in_=tpb[t][:].rearrange("p (k n) -> p k n", k=KC), func=AF.Copy),
                      reads=[f"tp{t}"], writes=[f"xnT{i}"])
            else:
                S_.op("dve", lambda e: e.tensor_copy(xnT[:, :, i * 128:(i + 1) * 128], tpb[t][:].rearrange("p (k n) -> p k n", k=KC)),
                      reads=[f"tp{t}"], writes=[f"xnT{i}"])

        return [s0, s1, s2, s3, s4, s5]

    def rn_stages(st, ps_fn, nh, dst_tile, dst_res, idx):
        par = idx % 2
        r4 = idx % 4
        w = nh * 64
        rn = small[:, 8 + 8 * r4:8 + 8 * r4 + nh]

        def s1():
            S_.op("act", lambda e: e.activation(out=sqb[par][:, 0:w], in_=ps_fn(), func=AF.Square), reads=[f"mm{st['m']}"], writes=R_sqb[par])

        def s2():
            S_.op("dve", lambda e: e.tensor_reduce(out=rn, in_=sqb[par][:, 0:w].rearrange("p (h d) -> p h d", h=nh), axis=AX.X, op=ALU.add),
                  reads=R_sqb[par], writes=[f"rn{r4}"])
            S_.op("dve", lambda e: e.tensor_scalar(out=rn, in0=rn, scalar1=1.0 / HD, scalar2=EPS, op0=ALU.mult, op1=ALU.add),
                  reads=[f"rn{r4}"], writes=[f"rn{r4}"])

        def s3():
            S_.op("act", lambda e: e.activation(out=rn, in_=rn, func=AF.Ln), reads=[f"rn{r4}"], writes=[f"rn{r4}"])
            S_.op("act", lambda e: e.activation(out=rn, in_=rn, func=AF.Exp, scale=-0.5), reads=[f"rn{r4}"], writes=[f"rn{r4}"])

        def s4():
            S_.op("dve", lambda e: e.tensor_tensor(out=dst_tile[:, :, 0:64], in0=ps_fn().rearrange("p (h d) -> p h d", h=nh),
                                                   in1=rn.unsqueeze(2).to_broadcast([128, nh, 64]), op=ALU.mult),
                  reads=[f"mm{st['m']}", f"rn{r4}"], writes=dst_res)

        return s1, s2, s3, s4

    item_ctr = [0]
    for q in range(NSEQ):
        for c in range(NCH):
            first_chunk = (q == 0 and c == 0)
            if first_chunk:
                for i in range(4):
                    r0 = q * S + (4 * c + i) * 128
                    S_.dma("sp", d_x[i], xh[:, i, :], x[r0:r0 + 128, :], writes=[f"xh{i}"])
            pipe = [norm_item(i, "gbuf", evac=("act" if i < 2 else "dve")) for i in range(4)] + [None]

            def i4_item(i):
                blk = 4 * c + i
                rb = blk % 8
                idx = item_ctr[0]
                item_ctr[0] += 1
                par = idx % 2
                st = {}
                fb = 64 + (i % 4) * 24
                z = small[:, fb:fb + 8]
                a_ = small[:, fb + 8:fb + 16]
                mn = small[:, fb + 16:fb + 24]
                fr = f"fg{i % 4}"

                def s0():
                    if i == 0:
                        st_sl["sl"] = ws_get(SL_I[4])
                    sl = st_sl["sl"]
                    m = next_mm()
                    st["m"] = m
                    ps = mmb[m]
                    S_.group("pe", [lambda e, kc=kc: e.matmul(ps[:, 0:264], xnT[:, kc, i * 128:(i + 1) * 128],
                                                               wslot[sl][:, kc * 512:kc * 512 + 264], start=(kc == 0), stop=(kc == KC - 1))
                                    for kc in range(KC)], reads=[f"xnT{i}", f"wslot{sl}"], writes=[f"mm{m}"])
                    if i == 3:
                        ws_done()

                r1, r2, r3, r4_ = rn_stages(st, lambda: mmb[st["m"]][:, 0:128], NKV, kAt[par], [f"kAt{par}"], idx)

                def s1():
                    r1()
                    ps = mmb[st["m"]]
                    S_.op("act", lambda e: e.activation(out=V_A[:, rb, :].rearrange("p (g c) -> p g c", c=65)[:, :, 0:64],
                                                        in_=ps[:, 128:256].rearrange("p (g d) -> p g d", g=NKV), func=AF.Copy),
                          reads=[f"mm{st['m']}", "V_A_ones"], writes=[f"V_A{rb}"])
                    S_.op("dve", lambda e: e.tensor_tensor(out=z, in0=ps[:, 256:264], in1=bfg[:], op=ALU.add), reads=[f"mm{st['m']}", "bfg"], writes=[fr + "z"])
                    S_.op("dve", lambda e: e.scalar_tensor_tensor(out=a_, in0=z, scalar=-1.0, in1=z, op0=ALU.mult, op1=ALU.max), reads=[fr + "z"], writes=[fr + "a"])
                    S_.op("dve", lambda e: e.tensor_single_scalar(out=mn, in_=z, scalar=0.0, op=ALU.min), reads=[fr + "z"], writes=[fr + "m"])

                def s2():
                    r2()
                    S_.op("act", lambda e: e.activation(out=a_, in_=a_, func=AF.Exp, scale=-1.0), reads=[fr + "a"], writes=[fr + "a"])
                    S_.op("act", lambda e: e.activation(out=a_, in_=a_, func=AF.Ln, bias=1.0), reads=[fr + "a"], writes=[fr + "a"])

                def s3():
                    r3()
                    S_.op("dve", lambda e: e.tensor_tensor(out=mn, in0=mn, in1=a_, op=ALU.subtract), reads=[fr + "m", fr + "a"], writes=[fr + "m"])

                def s4():
                    r4_()
                    fns = [lambda e: e.matmul(misc[:, 0:NHB], tri[:], mn, start=True, stop=(blk == 0))]
                    rds = ["tri", fr + "m"]
                    for j in range(i):
                        fbj = 64 + (j % 4) * 24
                        last = (j == i - 1) and (c == 0)
                        fns.append(lambda e, fbj=fbj, last=last: e.matmul(misc[:, 0:NHB], ones_f[:], small[:, fbj + 16:fbj + 24], start=False, stop=last))
                        rds += [f"fg{j % 4}m", "ones_f"]
                    if c > 0:
                        fns.append(lambda e: e.matmul(misc[:, 0:NHB], sel[:], cprev[:], start=False, stop=True))
                        rds += ["sel", "cprev"]
                    S_.group("pe", fns, reads=rds, writes=["misc"])
                    S_.op("dve", lambda e: e.tensor_copy(cchunk[:, i, :], misc[:, 0:NHB]), reads=["misc"], writes=[f"cc{i}"])
                    S_.op("dve", lambda e: e.tensor_scalar(out=negc[:, blk, :], in0=cchunk[:, i, :], scalar1=-1.0, scalar2=None, op0=ALU.mult),
                          reads=[f"cc{i}"], writes=[f"negc{blk}"])

                def s5():
                    t = next_tp()
                    st["t"] = t
                    tpv = tpb[t][:].rearrange("p (k n) -> p k n", k=KC)
                    S_.group("pe", [lambda e, g=g: e.transpose(tpv[0:66, g, :], kAt[par][:, g, :], ident_bf[:]) for g in range(NKV)],
                             reads=[f"kAt{par}", "ident_bf"], writes=[f"tp{t}"])

                def s6():
                    t = st["t"]
                    tpv = tpb[t][:].rearrange("p (k n) -> p k n", k=KC)
                    S_.op("act", lambda e: e.activation(out=KT_A[0:66, :, rb * 128:(rb + 1) * 128], in_=tpv[0:66, 0:NKV, :], func=AF.Copy),
                          reads=[f"tp{t}"], writes=[f"KT_A{rb}"])

                return [s0, s1, s2, s3, s4, s5, s6]

            def qk_item(slab_id, kind, i):
                blk = 4 * c + i
                idx = item_ctr[0]
                item_ctr[0] += 1
                par = idx % 2
                st = {}

                def s0():
                    if i == 0:
                        st_sl["sl"] = ws_get(slab_id)
                    sl = st_sl["sl"]
                    m = next_mm()
                    st["m"] = m
                    ps = mmb[m]
                    S_.group("pe", [lambda e, kc=kc: e.matmul(ps[:], xnT[:, kc, i * 128:(i + 1) * 128],
                                                               wslot[sl][:, kc * 512:(kc + 1) * 512], start=(kc == 0), stop=(kc == KC - 1))
                                    for kc in range(KC)], reads=[f"xnT{i}", f"wslot{sl}"], writes=[f"mm{m}"])
                    if i == 3:
                        ws_done()

                if kind == "vB":
                    def s1v():
                        ps = mmb[st["m"]]
                        S_.op("act", lambda e: e.activation(out=V_B[:, blk, :].rearrange("p (h c) -> p h c", c=65)[:, :, 0:64],
                                                            in_=ps[:].rearrange("p (h d) -> p h d", h=NHB), func=AF.Copy),
                              reads=[f"mm{st['m']}", "V_B_ones"], writes=[f"V_B{blk}"])
                    return [s0, s1v]
                tile_ = {"qA": qAt, "qB": qBt, "kB": kBt}[kind][par]
                tname = f"{kind}t{par}"
                nrow = 66 if kind == "qA" else 68
                s1, s2, s3, r4_ = rn_stages(st, lambda: mmb[st["m"]][:], 8, tile_, [tname], idx)

                def s4():
                    r4_()
                    if kind == "qB":
                        S_.op("dve", lambda e: e.tensor_copy(tile_[:, :, 64:65], cchunk[:, i, :].unsqueeze(2)), reads=[f"cc{i}"], writes=[tname])
                    if kind == "kB":
                        t1 = small[:, 160 + 16 * par:168 + 16 * par]
                        t2 = small[:, 168 + 16 * par:176 + 16 * par]
                        ng = negc[:, blk, :]
                        col = lambda k_: tile_[:, :, k_:k_ + 1].rearrange("p h o -> p (h o)")
                        S_.op("dve", lambda e: e.tensor_copy(col(65), ng), reads=[f"negc{blk}"], writes=[tname])
                        S_.op("dve", lambda e: e.tensor_tensor(out=t1, in0=ng, in1=col(65), op=ALU.subtract), reads=[f"negc{blk}", tname], writes=[f"spl{par}a"])
                        S_.op("dve", lambda e: e.tensor_copy(col(66), t1), reads=[f"spl{par}a"], writes=[tname])
                        S_.op("dve", lambda e: e.tensor_tensor(out=t2, in0=t1, in1=col(66), op=ALU.subtract), reads=[f"spl{par}a", tname], writes=[f"spl{par}b"])
                        S_.op("dve", lambda e: e.tensor_copy(col(67), t2), reads=[f"spl{par}b"], writes=[tname])

                def s5():
                    t = next_tp()
                    st["t"] = t
                    tpv = tpb[t][:].rearrange("p (k n) -> p k n", k=KC)
                    S_.group("pe", [lambda e, h=h: e.transpose(tpv[0:nrow, h, :], tile_[:, h, :], ident_bf[:]) for h in range(8)],
                             reads=[tname, "ident_bf"], writes=[f"tp{t}"])

                def s6():
                    t = st["t"]
                    tpv = tpb[t][:].rearrange("p (k n) -> p k n", k=KC)
                    if kind == "qA":
                        S_.op("act", lambda e: e.activation(out=QT_A[0:66, :, i, :].rearrange("p g (h n) -> p g h n", h=4),
                                                            in_=tpv[0:66, :, :].rearrange("p (g h) n -> p g h n", g=NKV),
                                                            func=AF.Copy, scale=scaleA[0:66, :]),
                              reads=[f"tp{t}", "scaleA"], writes=[f"QT_A{i}"])
                    elif kind == "qB":
                        S_.op("act", lambda e: e.activation(out=QT_B[0:68, :, i * 128:(i + 1) * 128], in_=tpv[0:68, :, :],
                                                            func=AF.Copy, scale=scaleB[0:68, :]),
                              reads=[f"tp{t}", "scaleB"], writes=[f"QT_B{i}"])
                    else:
                        S_.op("dve", lambda e: e.tensor_copy(KT_B[0:68, :, blk * 128:(blk + 1) * 128], tpv[0:68, :, :]),
                              reads=[f"tp{t}"], writes=[f"KT_B{blk}"])

                return [s0, s1, s2, s3, s4, s5, s6]

            st_sl = {}
            pipe += [i4_item(i) for i in range(4)]
            for slab_id, kind in ((SL_I[0], "qA"), (SL_I[2], "kB"), (SL_I[1], "qB"), (SL_I[3], "vB")):
                pipe += [qk_item(slab_id, kind, i) for i in range(4)]
            run_pipe(pipe, 7)
            S_.op("dve", lambda e: e.tensor_copy(cprev[:], cchunk[:, 3, :]), reads=["cc3"], writes=["cprev"])
            S_.dma("sp", d_g, gbuf[:], gM_d[:], writes=["gbuf"])

            items = []
            for h in range(NHB):
                NJ = 4 * c + 4
                for j in range(NJ):
                    items.append(dict(kind="B", h=h, j=j, first=(j == 0), last=(j == NJ - 1)))
            for i in range(4):
                blk = 4 * c + i
                for g in range(NKV):
                    pairs = ([(blk - 1, "P")] if blk > 0 else []) + [(blk, "C")]
                    for pi_, (kb, mk) in enumerate(pairs):
                        items.append(dict(kind="A", i=i, g=g, kb=kb, mk=mk, first=(pi_ == 0), last=(pi_ == len(pairs) - 1)))
            units = []
            for it in items:
                key = ("A", it["i"], it["g"]) if it["kind"] == "A" else ("B", it["h"])
                it["key"] = key
                if units and len(units[-1]) == 1 and units[-1][0]["key"] == key:
                    units[-1].append(it)
                else:
                    units.append([it])
            pend_A = []
            ag_i = [0]
            pend_fin = []
            oa_i = [0]
            un_i = [0]
            bc_i = [0]
            rd_i = [0]
            cur_oa = {}

            def acc(m):
                return (mmb[4], "mm4") if m == 4 else (tp1_f32, "tp1")

            def stage_S(unit):
                u = un_i[0] % 2
                u3 = un_i[0] % 3
                un_i[0] += 1
                widths = []
                for e_, it in enumerate(unit):
                    m = 2 * u + e_
                    it["pt"] = 2 * u3 + e_
                    ps = mmb[m]
                    if it["kind"] == "A":
                        i, g, kb = it["i"], it["g"], it["kb"]
                        rb = kb % 8
                        mask = maskP[:, g, :] if it["mk"] == "P" else maskC[:]
                        S_.group("pe", [lambda e: e.matmul(ps[:], KT_A[0:66, g, rb * 128:(rb + 1) * 128], QT_A[0:66, g, i, :], start=True, stop=False),
                                        lambda e: e.matmul(ps[:], ident_bf[:], mask, start=False, stop=True)],
                                 reads=[f"KT_A{rb}", f"QT_A{i}", "ident_bf", "maskP", "maskC"], writes=[f"mm{m}"])
                        it["N"] = 512
                        it["off"] = 0
                    else:
                        h, j = it["h"], it["j"]
                        jj = j - 4 * c
                        off = max(jj, 0) * 128
                        N = 512 - off
                        fns = [lambda e: e.matmul(ps[:, 0:N], KT_B[0:68, h, j * 128:(j + 1) * 128], QT_B[0:68, h, off:512], start=True, stop=(jj < 0))]
                        if jj >= 0:
                            fns.append(lambda e: e.matmul(ps[:, 0:128], ident_bf[:], maskC[:, 0:128], start=False, stop=True))
                        S_.group("pe", fns, reads=[f"KT_B{j}"] + [f"QT_B{ii}" for ii in range(4)] + ["ident_bf", "maskC"], writes=[f"mm{m}"])
                        it["N"] = N
                        it["off"] = off
                    widths.append(it["N"])
                if len(unit) == 2 and widths[0] == 512:
                    W = 512 + widths[1]
                    S_.op("act", lambda e: e.activation(out=PTp[u3][:, 0:W], in_=pairb[u][:, 0:W], func=AF.Exp),
                          reads=[f"mm{2 * u}", f"mm{2 * u + 1}"], writes=R_PT[2 * u3] + R_PT[2 * u3 + 1])
                else:
                    for e_, it in enumerate(unit):
                        k = 2 * u + e_
                        kp = 2 * u3 + e_
                        N = it["N"]
                        S_.op("act", lambda e, k=k, kp=kp, N=N: e.activation(out=PT[kp][:, 0:N], in_=mmb[k][:, 0:N], func=AF.Exp),
                              reads=[f"mm{k}"], writes=R_PT[kp])

            def stage_PV(it):
                key = it["key"]
                if it["first"]:
                    cur_oa[key] = (4, 5)[oa_i[0] % 2]
                    oa_i[0] += 1
                    for ent in [p for p in pend_fin if cur_oa[p[2]] == cur_oa[key] and p[2] != key]:
                        pend_fin.remove(ent)
                        stage_fin2(ent[1])
                m = cur_oa[key]
                oa, bank = acc(m)
                k = it["pt"]
                N, off = it["N"], it["off"]
                if it["kind"] == "A":
                    rb = it["kb"] % 8
                    g = it["g"]
                    S_.group("pe", [lambda e, hh=hh: e.matmul(oa[:, hh * 65:(hh + 1) * 65], PT[k][:, hh * 128:(hh + 1) * 128],
                                                               V_A[:, rb, g * 65:(g + 1) * 65], start=(it["first"] and hh == 0), stop=it["last"],
                                                               skip_group_check=True)
                                    for hh in range(4)],
                             reads=[f"V_A{rb}", "V_A_ones"] + R_PT[k], writes=[bank])
                else:
                    h, j = it["h"], it["j"]
                    S_.op("pe", lambda e: e.matmul(oa[0:65, off:512], V_B[:, j, h * 65:(h + 1) * 65], PT[k][:, 0:N], start=it["first"], stop=it["last"]),
                          reads=[f"V_B{j}", "V_B_ones"] + R_PT[k], writes=[bank])

            RDP = [0, 32, 64]

            def stage_finA1(it):
                key = ("A", it["i"], it["g"])
                oa, bank = acc(cur_oa[key])
                g = it["g"]
                r = ag_i[0] % 2
                ag_i[0] += 1
                it["ag"] = r
                den4 = small[:, 40 + 8 * r:44 + 8 * r]
                oav = oa[:, 0:260].rearrange("p (h c) -> p h c", c=65)
                S_.op("dve", lambda e: e.tensor_tensor(out=den4, in0=oav[:, :, 64], in1=sinkb[:, g * 4:(g + 1) * 4], op=ALU.add),
                      reads=[bank, "sinkb"], writes=[f"den4{r}"])
                S_.op("dve", lambda e: e.reciprocal(den4, den4), reads=[f"den4{r}"], writes=[f"den4{r}"])
                S_.op("dve", lambda e: e.tensor_tensor(out=mtok[r][:].rearrange("p (h d) -> p h d", h=4), in0=oav[:, :, 0:64],
                                                       in1=den4.unsqueeze(2).to_broadcast([128, 4, 64]), op=ALU.mult),
                      reads=[bank, f"den4{r}"], writes=R_mtok[r])

            def stage_finA2(it):
                i, g = it["i"], it["g"]
                r = it["ag"]
                t = 0
                S_.group("pe", [lambda e, a=a: e.transpose(tpb[t][:, a * 128:(a + 1) * 128], mtok[r][:, a * 128:(a + 1) * 128], ident_bf[:])
                                for a in range(2)], reads=R_mtok[r] + ["ident_bf"], writes=[f"tp{t}"])
                S_.op("dve", lambda e: e.tensor_copy(mixT[:, g * 2:g * 2 + 2, i * 128:(i + 1) * 128],
                                                     tpb[t][:, 0:256].rearrange("p (a n) -> p a n", a=2)),
                      reads=[f"tp{t}"], writes=[f"mixT{g * 2}", f"mixT{g * 2 + 1}"])

            def stage_fin1(it):
                key = ("B", it["h"])
                oa, bank = acc(cur_oa[key])
                r = rd_i[0] % 3
                rd_i[0] += 1
                it["rd"] = r
                p0 = RDP[r]
                rrow = rden[p0:p0 + 1, :]
                if 4 * c + 4 <= 8 or it["h"] == NHB - 1:
                    S_.op("act", lambda e: e.activation(out=rrow, in_=oa[64:65, :], func=AF.Ln), reads=[bank], writes=[f"BC|rden{r}|sh4"])
                    S_.op("act", lambda e: e.activation(out=rrow, in_=rrow, func=AF.Exp, scale=-1.0), reads=[f"BC|rden{r}|sh4"], writes=[f"BC|rden{r}|sh4"])
                else:
                    S_.op("dve", lambda e: e.reciprocal(rrow, oa[64:65, :]), reads=[bank], writes=[f"BC|rden{r}|sh4"])

            def stage_fin2(it):
                key = ("B", it["h"])
                oa, bank = acc(cur_oa[key])
                b = 0
                r = it["rd"]
                p0 = RDP[r]
                S_.op("pe", lambda e: e.matmul(misc[0:64, :], ones_f[p0:p0 + 1, 0:64], rden[p0:p0 + 1, :], start=True, stop=True),
                      reads=["ones_f", f"BC|rden{r}|sh4"], writes=["misc"])
                S_.op("dve", lambda e: e.tensor_copy(bc_sb[b][0:64, :], misc[0:64, :]), reads=["misc"], writes=R_bc[b])
                h = it["h"]
                par = h % 2
                ec = 4 + h // 2
                S_.op("dve", lambda e: e.tensor_tensor(out=mixT[par * 64:(par + 1) * 64, ec, :], in0=oa[0:64, :], in1=bc_sb[b][0:64, :], op=ALU.mult),
                      reads=[bank] + R_bc[b], writes=[f"mixT{ec}"])

            FDELAY = 4
            n_u = len(units)
            k = 0
            while k < n_u + 2 or pend_fin or pend_A:
                if k < n_u:
                    stage_S(units[k])
                if 0 <= k - 2 < n_u:
                    for it in units[k - 2]:
                        stage_PV(it)
                        if it["last"]:
                            if it["kind"] == "A":
                                while len(pend_A) > 1:
                                    stage_finA2(pend_A.pop(0)[1])
                                stage_finA1(it)
                                pend_A.append((k + 2, it))
                            else:
                                while len(pend_fin) > 2:
                                    stage_fin2(pend_fin.pop(0)[1])
                                stage_fin1(it)
                                pend_fin.append((k + FDELAY, it, it["key"]))
                while pend_fin and pend_fin[0][0] <= k:
                    stage_fin2(pend_fin.pop(0)[1])
                while pend_A and pend_A[0][0] <= k:
                    stage_finA2(pend_A.pop(0)[1])
                k += 1

            def op_item(dh, i):
                st = {}

                def s0():
                    if i == 0 and dh == 0:
                        st_sl["o"] = [ws_get(SL_O[0]), ws_get(SL_O[1], ahead=1)]
                    sl = st_sl["o"][dh]
                    m = next_mm()
                    st["m"] = m
                    ps = mmb[m]
                    S_.group("pe", [lambda e, ec=ec: e.matmul(ps[:], mixT[:, ec, i * 128:(i + 1) * 128], wslot[sl][:, ec * 512:(ec + 1) * 512],
                                                               start=(ec == 0), stop=(ec == KC - 1)) for ec in range(KC)],
                             reads=[f"mixT{ec}" for ec in range(KC)] + [f"wslot{sl}"], writes=[f"mm{m}"])
                    if i == 3 and dh == 1:
                        ws_done()
                        ws_done()

                def s1():
                    ps = mmb[st["m"]]
                    S_.op("dve", lambda e: e.tensor_tensor(out=xh[:, i, dh * 512:(dh + 1) * 512], in0=ps[:], in1=xh[:, i, dh * 512:(dh + 1) * 512], op=ALU.add),
                          reads=[f"mm{st['m']}", f"xh{i}"], writes=[f"xh{i}"])

                return [s0, s1]

            o = {(i, dh): op_item(dh, i) for i in range(4) for dh in range(2)}
            n_ = [norm_item(i, "gbuf", evac="act") for i in range(4)]
            pipe = [o[0, 0], o[0, 1], n_[0], o[1, 0], o[1, 1], n_[1], o[2, 0], o[2, 1], n_[2], o[3, 0], o[3, 1], n_[3]]
            run_pipe(pipe, 6)
            last_chunk = (q == NSEQ - 1 and c == NCH - 1)
            if not last_chunk:
                S_.dma("sp", d_g, gbuf[:], gA_d[:], writes=["gbuf"])

            def ffn_up(g):
                sl = ws_get(SL_U[g])
                hb = g % 2
                ms = [next_mm() for _ in range(4)]
                parts = [(0, 384, [0, 1, 2]), (384, 512, [3])] if g == 0 else [(0, 512, [0, 1, 2, 3])]
                for c0, c1, tiles in parts:
                    for fc in range(4):
                        ps = mmb[ms[fc]]
                        S_.group("pe", [lambda e, kc=kc: e.matmul(ps[:, c0:c1], wslot[sl][:, kc * 512 + fc * 128:kc * 512 + (fc + 1) * 128],
                                                                   xnT[:, kc, c0:c1], start=(kc == 0), stop=(kc == KC - 1)) for kc in range(KC)],
                                 reads=[f"xnT{i}" for i in tiles] + [f"wslot{sl}"], writes=[f"mm{ms[fc]}"])
                for fc in range(4):
                    m = ms[fc]
                    ps = mmb[m]
                    rb_ = fc % 2
                    S_.op("act", lambda e: e.activation(out=rtmp[rb_][:], in_=ps[:], func=AF.Relu), reads=[f"mm{m}"], writes=R_rtmp[rb_])
                    S_.op("pool", lambda e: e.tensor_tensor(out=hid[hb][:, fc, :], in0=rtmp[rb_][:], in1=rtmp[rb_][:], op=ALU.mult),
                          reads=R_rtmp[rb_], writes=R_hid[hb])
                ws_done()

            def ffn_down(g):
                sl = ws_get(SL_D[g])
                hb = g % 2
                nq, ncn = (q, c + 1) if c + 1 < NCH else (q + 1, 0)
                for i in range(4):
                    for dh in range(2):
                        m = next_mm()
                        ps = mmb[m]
                        S_.group("pe", [lambda e, fc=fc: e.matmul(ps[:], hid[hb][:, fc, i * 128:(i + 1) * 128],
                                                                   wslot[sl][:, fc * 1024 + dh * 512:fc * 1024 + (dh + 1) * 512],
                                                                   start=(fc == 0), stop=(fc == 3)) for fc in range(4)],
                                 reads=R_hid[hb] + [f"wslot{sl}"], writes=[f"mm{m}"])
                        if g < NG - 1:
                            S_.op("dve", lambda e: e.tensor_tensor(out=xh[:, i, dh * 512:(dh + 1) * 512], in0=ps[:], in1=xh[:, i, dh * 512:(dh + 1) * 512], op=ALU.add),
                                  reads=[f"mm{m}", f"xh{i}"], writes=[f"xh{i}"])
                        else:
                            yb = (2 * i + dh) % 4
                            r0 = q * S + (4 * c + i) * 128
                            S_.op("dve", lambda e: e.tensor_tensor(out=yout[yb][:], in0=ps[:], in1=xh[:, i, dh * 512:(dh + 1) * 512], op=ALU.add),
                                  reads=[f"mm{m}", f"xh{i}"], writes=R_yout[yb])
                            S_.dma("sp", d_yb[yb], y[r0:r0 + 128, dh * 512:(dh + 1) * 512], yout[yb][:], reads=R_yout[yb], writes=[f"y{yb}"])
                    if g == NG - 1 and nq < NSEQ:
                        r1 = nq * S + (4 * ncn + i) * 128
                        S_.dma("sp", d_x[i], xh[:, i, :], x[r1:r1 + 128, :], writes=[f"xh{i}"])
                ws_done()

            for g in range(NG + 1):
                if g < NG:
                    ffn_up(g)
                if g >= 1:
                    ffn_down(g - 1)

    S_.wait_all("sp")
    return nc, S_


_PROG_CACHE = {}


def make_in_maps(x2d_list, w_in, w_out, w_up, w_down, attn_norm_g, mlp_norm_g, b_forget, q_norm_a, k_norm_a,
                 sink_logits, q_norm_b, k_norm_b):
    c = host_consts()
    f = lambda a: np.ascontiguousarray(np.asarray(a, dtype=np.float32))
    common = {
        "w_in": f(w_in), "w_out": f(w_out), "w_up": f(w_up), "w_down": f(w_down),
        "gA": f(np.broadcast_to(np.asarray(attn_norm_g)[None, :], (128, D))),
        "gM": f(np.broadcast_to(np.asarray(mlp_norm_g)[None, :], (128, D))),
        "bfg": f(np.broadcast_to(np.asarray(b_forget)[None, :], (128, NHB))),
        "gains": f(np.stack([np.asarray(q_norm_a), np.asarray(k_norm_a), np.asarray(q_norm_b), np.asarray(k_norm_b)], axis=1)),
        "sinks": f(np.broadcast_to(np.asarray(sink_logits)[None, :], (128, NHA))),
        **{k: f(v) for k, v in c.items()},
    }
    return [{"x": f(xs), **common} for xs in x2d_list]


def kernel(x, attn_norm_g, w_in, b_forget, q_norm_a, k_norm_a, sink_logits, q_norm_b, k_norm_b, w_out,
           mlp_norm_g, w_up, w_down):
    x = np.asarray(x)
    B, S, _ = x.shape
    DFF = np.asarray(w_up).shape[1]
    n = 8
    per = B // n
    key = (per, S, DFF)
    if key not in _PROG_CACHE:
        _PROG_CACHE[key] = build_program(per, S, DFF)[0]
    nc = _PROG_CACHE[key]
    shards = [x[i * per:(i + 1) * per].reshape(per * S, D) for i in range(n)]
    in_maps = make_in_maps(shards, w_in, w_out, w_up, w_down, attn_norm_g, mlp_norm_g, b_forget, q_norm_a, k_norm_a,
                           sink_logits, q_norm_b, k_norm_b)
    res = run_bass_kernel_spmd(nc, in_maps, core_ids=list(range(n)))
    out = np.concatenate([np.asarray(r["y"]).reshape(per, S, D) for r in res.results], axis=0)
    return out.astype(np.float32)
```

```python
import numpy as np
import concourse.bass as bass
import concourse.mybir as mybir
from concourse.bass_utils import run_bass_kernel_spmd

F32 = mybir.dt.float32
BF16 = mybir.dt.bfloat16
ALU = mybir.AluOpType
AF = mybir.ActivationFunctionType
AX = mybir.AxisListType

SEM_ROLL = 30000
D = 1024
KC = 8
HD = 64
NHA = 8
NKV = 2
NHB = 8
INW = 2312
EPS = 1e-6
NEG = -30000.0
NSLOT = 3


class DmaSem:
    def __init__(self, nc, name):
        self.nc = nc
        self.name = name
        self.sem = nc.alloc_semaphore(name)
        self.count = 0


class Sched:
    def __init__(self, nc):
        self.nc = nc
        self.eng = {"pe": nc.tensor, "act": nc.scalar, "dve": nc.vector, "pool": nc.gpsimd, "sp": nc.sync}
        self.sem = {}
        self.cnt = {}
        self.nsem = 0
        for e in ("pe", "act", "dve", "pool"):
            self._new_sem(e)
        self.waited = {e: {} for e in self.eng}
        self.res = {}
        self.n_ops = 0
        self.sh_owner = {}

    def _expand(self, reads, writes):
        r2, w2 = [], []
        for lst, is_w in ((reads, False), (writes, True)):
            for name in lst:
                if "|" in name:
                    fam, fine, gr = name.split("|")
                    (w2 if is_w else r2).append(fine)
                    for g in gr.split(","):
                        if self.sh_owner.get(g) != fam:
                            self.sh_owner[g] = fam
                            w2.append(g)
                        else:
                            r2.append(g)
                else:
                    (w2 if is_w else r2).append(name)
        return r2, w2

    def _new_sem(self, e):
        self.sem[e] = self.nc.alloc_semaphore(f"s_{e}_{self.nsem}")
        self.nsem += 1
        self.cnt[e] = 0

    def _wait(self, e, tok):
        if tok is None:
            return
        sem, val, src = tok
        if src == e and e == "pe":
            return
        w = self.waited[e]
        if w.get(sem.name, 0) >= val:
            return
        w[sem.name] = val
        self.eng[e].wait_ge(sem, val)

    @staticmethod
    def _is_psum(r):
        return r.startswith("mm") or r.startswith("tp") or r == "misc"

    def deps(self, e, reads, writes):
        for r in reads:
            st = self.res.get(r)
            if st:
                self._wait(e, st["w"])
                if self._is_psum(r):
                    for t in st["r"].values():
                        if t[2] != e:
                            self._wait(e, t)
        for r in writes:
            st = self.res.get(r)
            if st:
                self._wait(e, st["w"])
                for t in st["r"].values():
                    self._wait(e, t)

    def commit(self, tok, reads, writes):
        for r in reads:
            st = self.res.setdefault(r, {"w": None, "r": {}})
            st["r"][tok[0].name] = tok
        for r in writes:
            self.res[r] = {"w": tok, "r": {}}

    def _signal(self, e, ins, reads, writes):
        if self.cnt[e] >= SEM_ROLL:
            self._new_sem(e)
        self.cnt[e] += 1
        ins.then_inc(self.sem[e], 1)
        tok = (self.sem[e], self.cnt[e], e)
        self.commit(tok, reads, writes)
        self.n_ops += 1
        return tok

    def op(self, e, fn, reads=(), writes=()):
        reads, writes = self._expand(reads, writes)
        self.deps(e, reads, writes)
        return self._signal(e, fn(self.eng[e]), reads, writes)

    def group(self, e, fns, reads=(), writes=()):
        reads, writes = self._expand(reads, writes)
        self.deps(e, reads, writes)
        ins = None
        for fn in fns:
            ins = fn(self.eng[e])
        return self._signal(e, ins, reads, writes)

    def dma(self, q, dsem, out, in_, reads=(), writes=(), **kw):
        reads, writes = self._expand(reads, writes)
        self.deps(q, reads, writes)
        ins = self.eng[q].dma_start(out=out, in_=in_, **kw)
        if dsem.count >= SEM_ROLL:
            dsem.sem = self.nc.alloc_semaphore(f"{dsem.name}_r{self.nsem}")
            self.nsem += 1
            dsem.count = 0
        dsem.count += 16
        ins.then_inc(dsem.sem, 16)
        tok = (dsem.sem, dsem.count, "dma")
        self.commit(tok, reads, writes)
        return tok

    def wait_all(self, e):
        best = {}
        for st in self.res.values():
            for t in [st["w"], *st["r"].values()]:
                if t is not None and (t[0].name not in best or best[t[0].name][1] < t[1]):
                    best[t[0].name] = t
        for t in best.values():
            self._wait(e, t)


def alibi_slopes():
    return [2.0 ** (-(h + 1)) for h in range(NHA)]


def host_consts():
    p = np.arange(128, dtype=np.float32)
    c = {}
    c["ident"] = np.eye(128, dtype=np.float32)
    c["tri"] = (p[:, None] <= p[None, :]).astype(np.float32)
    sel = np.zeros((128, 128), np.float32)
    sel[127, :] = 1.0
    c["sel"] = sel
    mc = np.where(p[:, None] <= p[None, :], 0.0, NEG).astype(np.float32)
    c["maskC"] = np.tile(mc, (1, 4))
    sl = alibi_slopes()
    mp = np.zeros((NKV, 128, 512), np.float32)
    for g in range(NKV):
        for hh in range(4):
            s_ = sl[g * 4 + hh]
            mp[g, :, hh * 128:(hh + 1) * 128] = np.where(p[:, None] > p[None, :], -128.0 * s_, NEG)
    c["maskP"] = mp
    aq = np.zeros((128, NHA, 2), np.float32)
    for h in range(NHA):
        aq[:, h, 0] = -sl[h] * p
        aq[:, h, 1] = sl[h]
    c["augqA"] = aq
    ak = np.zeros((128, 2), np.float32)
    ak[:, 0] = 1.0
    ak[:, 1] = p
    c["augkA"] = ak
    return c


def build_program(NSEQ, S, DFF):
    NB = S // 128
    NCH = S // 512
    NG = DFF // 512
    NTOK = NSEQ * S
    nc = bass.Bass("TRN2", target_bir_lowering=False)
    S_ = Sched(nc)

    def dram_in(name, shape, dt=F32):
        return nc.dram_tensor(name, shape, dt, kind="ExternalInput").ap()

    x = dram_in("x", [NTOK, D])
    w_in = dram_in("w_in", [D, INW])
    w_out = dram_in("w_out", [D, D])
    w_up = dram_in("w_up", [D, DFF])
    w_down = dram_in("w_down", [DFF, D])
    gA_d = dram_in("gA", [128, D])
    gM_d = dram_in("gM", [128, D])
    bfg_d = dram_in("bfg", [128, NHB])
    gains_d = dram_in("gains", [64, 4])
    sinks_d = dram_in("sinks", [128, NHA])
    ident_d = dram_in("ident", [128, 128])
    tri_d = dram_in("tri", [128, 128])
    sel_d = dram_in("sel", [128, 128])
    maskC_d = dram_in("maskC", [128, 512])
    maskP_d = dram_in("maskP", [NKV, 128, 512])
    augqA_d = dram_in("augqA", [128, NHA, 2])
    augkA_d = dram_in("augkA", [128, 2])
    y = nc.dram_tensor("y", [NTOK, D], F32, kind="ExternalOutput").ap()

    NSLAB = 5 + 2 + 2 * NG
    wscr = nc.dram_tensor("wscr", [NSLAB, 128, 4096], BF16).ap()
    SL_I = [0, 1, 2, 3, 4]
    SL_O = [5, 6]
    SL_U = [7 + 2 * g for g in range(NG)]
    SL_D = [8 + 2 * g for g in range(NG)]

    def sb(name, shape, dt):
        return nc.alloc_sbuf_tensor("sb_" + name, shape, dt)

    wslot = [sb(f"wslot{k}", [128, 4096], BF16) for k in range(NSLOT)]
    KT_B = sb("KT_B", [128, NHB, S], BF16)
    V_B = sb("V_B", [128, NB, NHB * 65], BF16)
    KT_A = sb("KT_A", [128, NKV, 8 * 128], BF16)
    V_A = sb("V_A", [128, 8, NKV * 65], BF16)
    xh = sb("xh", [128, 4, D], F32)
    xnT = sb("xnT", [128, KC, 512], BF16)
    QT_B = sb("QT_B", [128, NHB, 512], BF16)
    QT_A = sb("QT_A", [128, NKV, 4, 4 * 128], BF16)
    mixT = sb("mixT", [128, KC, 512], BF16)
    gbuf = sb("gbuf", [128, D], F32)
    ident_bf = sb("ident_bf", [128, 128], BF16)
    tri = sb("tri", [128, 128], F32)
    sel = sb("sel", [128, 128], F32)
    maskC = sb("maskC", [128, 512], BF16)
    maskP = sb("maskP", [128, NKV, 512], BF16)
    ones_f = sb("ones_f", [128, 128], F32)
    negc = sb("negc", [128, NB, NHB], F32)
    cchunk = sb("cchunk", [128, 4, NHB], F32)
    bfg = sb("bfg", [128, NHB], F32)
    gains = sb("gains", [64, 4], F32)
    scaleA = sb("scaleA", [128, 1], F32)
    scaleB = sb("scaleB", [128, 1], F32)
    sinkb = sb("sinkb", [128, NHA], F32)
    small = sb("small", [128, 192], F32)
    cprev = sb("cprev", [128, NHB], F32)
    shared = sb("shared", [128, 19456], mybir.dt.uint8)

    def view(off, shape, dt):
        nbytes = int(np.prod(shape[1:])) * (4 if dt == F32 else 2)
        ap = shared[:, off:off + nbytes].bitcast(dt)
        if len(shape) == 3:
            ap = ap.rearrange("p (a b) -> p a b", a=shape[1])
        return ap

    xpre = view(0, [128, 1024], F32)
    R_xpre = ["A|xpre|sh0,sh1"]
    stage = [view(0, [128, 512], F32), view(2048, [128, 512], F32)]
    sqb = [view(4096, [128, 512], F32), view(6144, [128, 512], F32)]
    junk = view(4096, [128, 1024], F32)
    xnb = [view(8192, [128, 1024], BF16), view(10240, [128, 1024], BF16)]
    qAt = [view(12288, [128, NHA, 66], BF16), view(12288 + 1056, [128, NHA, 66], BF16)]
    qBt = [view(14400, [128, NHB, 68], BF16), view(14400 + 1088, [128, NHB, 68], BF16)]
    kBt = [view(16576, [128, NHB, 68], BF16), view(16576 + 1088, [128, NHB, 68], BF16)]
    kAt = [view(18752, [128, NKV, 66], BF16), view(18752 + 264, [128, NKV, 66], BF16)]
    PT = [view(1024 * k, [128, 512], BF16) for k in range(6)]
    PTp = [view(2048 * k, [128, 1024], BF16) for k in range(3)]
    rden = view(8192, [128, 512], F32)
    bc_sb = [view(6144, [128, 512], F32)]
    mtok = [view(10240, [128, 256], BF16), view(10240 + 512, [128, 256], BF16)]
    yout = [view(off, [128, 512], F32) for off in (0, 2048, 8192, 10240)]
    R_yout = [["Y|yout0|sh0"], ["Y|yout1|sh1"], ["Y|yout2|sh4"], ["Y|yout3|sh5"]]
    hid = [view(0, [128, 4, 512], BF16), view(4096, [128, 4, 512], BF16)]
    rtmp = [view(8192, [128, 512], F32), view(10240, [128, 512], F32)]
    R_stage = [["A|stage0|sh0"], ["A|stage1|sh1"]]
    R_sqb = [["A|sqb0|sh2"], ["A|sqb1|sh3"]]
    R_junk = R_sqb[0] + R_sqb[1]
    R_xnb = [["A|xnb0|sh4"], ["A|xnb1|sh5"]]
    R_PT = [[f"BC|PT{k}|sh{k // 2}"] for k in range(6)]
    R_bc = [["BC|bc0|sh3"]]
    R_mtok = [["BC|mtok0|sh5"], ["BC|mtok1|sh5"]]
    R_hid = [["E|hid0|sh0,sh1"], ["E|hid1|sh2,sh3"]]
    R_rtmp = [["E|rtmp0|sh4"], ["E|rtmp1|sh5"]]

    pairb = [nc.alloc_psum_tensor(f"pair{k}", [128, 1024], F32) for k in range(2)]
    mm4 = nc.alloc_psum_tensor("mm4", [128, 512], F32)
    mmb = [pairb[0][:, 0:512], pairb[0][:, 512:1024], pairb[1][:, 0:512], pairb[1][:, 512:1024], mm4[:]]
    tpb = [nc.alloc_psum_tensor(f"tp{k}", [128, 1024], BF16) for k in range(2)]
    misc = nc.alloc_psum_tensor("misc", [128, 512], F32)
    tp1_f32 = tpb[1][:].bitcast(F32)
    mm_i = [0]

    def next_mm():
        k = mm_i[0] % 5
        mm_i[0] += 1
        return k

    tp_i = [0]

    def next_tp():
        k = tp_i[0] % 2
        tp_i[0] += 1
        return k

    d_const = DmaSem(nc, "d_const")
    d_constp = DmaSem(nc, "d_constp")
    d_aug = DmaSem(nc, "d_aug")
    d_scr = [DmaSem(nc, f"d_scr{k}") for k in range(NSLAB)]
    d_slot = [DmaSem(nc, f"d_slot{k}") for k in range(NSLOT)]
    d_x = [DmaSem(nc, f"d_x{k}") for k in range(4)]
    d_xp = [DmaSem(nc, f"d_xp{k}") for k in range(4)]
    d_yb = [DmaSem(nc, f"d_yb{k}") for k in range(4)]
    d_y = [DmaSem(nc, f"d_y{k}") for k in range(4)]
    d_g = DmaSem(nc, "d_g")

    cst_tmp = view(0, [128, 512], F32)
    cst_tmp2 = view(2048, [128, 1024], F32)

    def load_const(dst, src, name):
        S_.dma("sp", d_const, dst, src, writes=[name])

    load_const(tri[:], tri_d[:], "tri")
    load_const(sel[:], sel_d[:], "sel")
    load_const(bfg[:], bfg_d[:], "bfg")
    load_const(gains[:], gains_d[:], "gains")
    load_const(sinkb[:], sinks_d[:], "sinkb")
    load_const(gbuf[:], gA_d[:], "gbuf")
    S_.dma("pool", d_constp, ident_bf[:], ident_d[:], writes=["ident_bf"])
    S_.dma("pool", d_constp, maskC[:], maskC_d[:], writes=["maskC"])
    S_.dma("pool", d_constp, maskP[:], maskP_d.rearrange("g p n -> p g n"), writes=["maskP"])
    tok_const = (d_const.sem, d_const.count, "dma")
    for r in ["tri", "sel", "bfg", "gains", "sinkb", "gbuf"]:
        S_.res[r] = {"w": tok_const, "r": {}}
    tok_constp = (d_constp.sem, d_constp.count, "dma")
    for r in ["ident_bf", "maskC", "maskP"]:
        S_.res[r] = {"w": tok_constp, "r": {}}

    def load_aug_consts():
        for k in range(2):
            S_.dma("pool", d_aug, kAt[k][:, :, 64:66], augkA_d.unsqueeze(1).to_broadcast([128, NKV, 2]), writes=[f"kAt{k}"])
        for k in range(2):
            S_.dma("pool", d_aug, qAt[k][:, :, 64:66], augqA_d[:], writes=[f"qAt{k}"])
        tok_aug = (d_aug.sem, d_aug.count, "dma")
        for r in ["qAt0", "qAt1", "kAt0", "kAt1"]:
            S_.res[r] = {"w": tok_aug, "r": {}}

    def scr3(k, a):
        return wscr[k].rearrange("p (a b) -> p a b", a=a)

    w_in_v = w_in.rearrange("(kc p) n -> p kc n", p=128)
    w_out_v = w_out.rearrange("(kc p) n -> p kc n", p=128)
    w_up_v = w_up.rearrange("(kc p) n -> p kc n", p=128)
    w_down_v = w_down.rearrange("(fc p) n -> p fc n", p=128)

    def conv(k, dst, src):
        S_.dma("pool", d_scr[k], dst, src, writes=[f"scr{k}"])

    conv(4, scr3(4, KC)[:, :, 0:256], w_in_v[:, :, 512:768])
    conv(4, scr3(4, KC)[:, :, 256:264], w_in_v[:, :, 2304:2312])
    S_.res["scr4"]["w"] = (d_scr[4].sem, d_scr[4].count, "dma")
    load_aug_consts()
    conv(0, scr3(0, KC), w_in_v[:, :, 0:512])
    conv(2, scr3(2, KC), w_in_v[:, :, 1280:1792])
    conv(1, scr3(1, KC), w_in_v[:, :, 768:1280])
    conv(3, scr3(3, KC), w_in_v[:, :, 1792:2304])
    for dh in range(2):
        conv(SL_O[dh], scr3(SL_O[dh], KC), w_out_v[:, :, dh * 512:(dh + 1) * 512])
    for g in range(NG):
        conv(SL_U[g], scr3(SL_U[g], KC), w_up_v[:, :, g * 512:(g + 1) * 512])
        conv(SL_D[g], scr3(SL_D[g], 4), w_down_v[:, g * 4:(g + 1) * 4, :])

    S_.op("pool", lambda e: e.memset(ones_f[:], 1.0), writes=["ones_f"])
    S_.op("pool", lambda e: e.memset(scaleA[:], 1.0), writes=["scaleA"])
    S_.op("pool", lambda e: e.memset(scaleB[:], 1.0), writes=["scaleB"])
    S_.op("pool", lambda e: e.memset(V_B[:].rearrange("p n (h c) -> p (n h) c", c=65)[:, :, 64:65], 1.0), writes=["V_B_ones"])
    S_.op("pool", lambda e: e.memset(V_A[:].rearrange("p n (h c) -> p (n h) c", c=65)[:, :, 64:65], 1.0), writes=["V_A_ones"])
    for k in range(2):
        S_.op("pool", lambda e, k=k: e.memset(kBt[k][:, :, 64:65], 1.0), writes=[f"kBt{k}"])
        S_.op("pool", lambda e, k=k: e.memset(qBt[k][:, :, 65:68], 1.0), writes=[f"qBt{k}"])
    S_.op("dve", lambda e: e.scalar_tensor_tensor(out=scaleA[0:64, :], in0=gains[:, 0:1], scalar=0.125, in1=gains[:, 1:2],
                                                   op0=ALU.mult, op1=ALU.mult), reads=["gains"], writes=["scaleA"])
    S_.op("dve", lambda e: e.scalar_tensor_tensor(out=scaleB[0:64, :], in0=gains[:, 2:3], scalar=0.125, in1=gains[:, 3:4],
                                                   op0=ALU.mult, op1=ALU.mult), reads=["gains"], writes=["scaleB"])
    S_.op("act", lambda e: e.activation(out=sinkb[:], in_=sinkb[:], func=AF.Exp), reads=["sinkb"], writes=["sinkb"])

    wseq = []
    for _q in range(NSEQ):
        for _c in range(NCH):
            wseq += [SL_I[4], SL_I[0], SL_I[2], SL_I[1], SL_I[3]]
            wseq += SL_O
            order = []
            for g in range(NG + 1):
                if g < NG:
                    order.append(SL_U[g])
                if g >= 1:
                    order.append(SL_D[g - 1])
            wseq += order
    ws = {"pos": 0, "loaded": 0}

    def ws_load(i):
        s = i % NSLOT
        if wseq[i] == SL_I[4]:
            S_.dma("sp", d_slot[s], wslot[s][:].rearrange("p (a b) -> p a b", a=KC)[:, :, 0:264], scr3(wseq[i], KC)[:, :, 0:264],
                   reads=[f"scr{wseq[i]}"], writes=[f"wslot{s}"])
        else:
            S_.dma("sp", d_slot[s], wslot[s][:], wscr[wseq[i]], reads=[f"scr{wseq[i]}"], writes=[f"wslot{s}"])

    def ws_get(expect, ahead=0):
        i = ws["pos"] + ahead
        assert wseq[i] == expect, (i, wseq[i], expect)
        while ws["loaded"] <= i:
            ws_load(ws["loaded"])
            ws["loaded"] += 1
        return i % NSLOT

    def ws_done():
        ws["pos"] += 1
        nxt = ws["pos"] - 1 + NSLOT
        if nxt < len(wseq) and ws["loaded"] == nxt:
            ws_load(nxt)
            ws["loaded"] += 1

    for i in range(min(NSLOT, len(wseq))):
        ws_load(i)
        ws["loaded"] += 1

    def run_pipe(items, nst):
        n = len(items)
        for step in range(n + nst - 1):
            for s in reversed(range(nst)):
                t = step - s
                if 0 <= t < n and items[t] is not None and s < len(items[t]) and items[t][s] is not None:
                    items[t][s]()

    def norm_item(i, gname, evac="dve", pre=False):
        xi = xpre[:] if pre else xh[:, i, :]
        xres = R_xpre if pre else [f"xh{i}"]
        b = i % 2
        ssc = small[:, 4 + i:5 + i]
        rsc = small[:, i:i + 1]
        st = {}

        def s0():
            S_.op("act", lambda e: e.activation(out=junk[:], in_=xi, func=AF.Square, accum_out=ssc),
                  reads=xres, writes=R_junk + [f"ss{i}"])
            if pre:
                S_.op("dve", lambda e: e.tensor_copy(xh[:, i, :], xi), reads=xres, writes=[f"xh{i}"])

        def s1():
            S_.op("dve", lambda e: e.tensor_scalar(out=rsc, in0=ssc, scalar1=1.0 / D, scalar2=EPS, op0=ALU.mult, op1=ALU.add),
                  reads=[f"ss{i}"], writes=[f"rs{i}"])

        def s2():
            S_.op("act", lambda e: e.activation(out=rsc, in_=rsc, func=AF.Ln), reads=[f"rs{i}"], writes=[f"rs{i}"])
            S_.op("act", lambda e: e.activation(out=rsc, in_=rsc, func=AF.Exp, scale=-0.5), reads=[f"rs{i}"], writes=[f"rs{i}"])

        def s3():
            S_.op("dve", lambda e: e.scalar_tensor_tensor(out=xnb[b][:], in0=xi, scalar=rsc, in1=gbuf[:], op0=ALU.mult, op1=ALU.mult),
                  reads=xres + [f"rs{i}", gname], writes=R_xnb[b])

        def s4():
            t = next_tp()
            st["t"] = t
            S_.group("pe", [lambda e, kc=kc: e.transpose(tpb[t][:, kc * 128:(kc + 1) * 128], xnb[b][:, kc * 128:(kc + 1) * 128], ident_bf[:])
                            for kc in range(KC)], reads=R_xnb[b] + ["ident_bf"], writes=[f"tp{t}"])

        def s5():
            t = st["t"]
            if evac == "act":
                S_.op("act", lambda e: e.activation(out=xnT[:, :, i * 128:(i + 1) * 128], in_=tpb[t][:].rearrange("p (k n) -> p k n", k=KC), func=AF.Copy),
                      reads=[f"tp{t}"], writes=[f"xnT{i}"])
            else:
                S_.op("dve", lambda e: e.tensor_copy(xnT[:, :, i * 128:(i + 1) * 128], tpb[t][:].rearrange("p (k n) -> p k n", k=KC)),
                      reads=[f"tp{t}"], writes=[f"xnT{i}"])

        return [s0, s1, s2, s3, s4, s5]

    def rn_stages(st, ps_fn, nh, dst_tile, dst_res, idx):
        par = idx % 2
        r4 = idx % 4
        w = nh * 64
        rn = small[:, 8 + 8 * r4:8 + 8 * r4 + nh]

        def s1():
            S_.op("act", lambda e: e.activation(out=sqb[par][:, 0:w], in_=ps_fn(), func=AF.Square), reads=[f"mm{st['m']}"], writes=R_sqb[par])

        def s2():
            S_.op("dve", lambda e: e.tensor_reduce(out=rn, in_=sqb[par][:, 0:w].rearrange("p (h d) -> p h d", h=nh), axis=AX.X, op=ALU.add),
                  reads=R_sqb[par], writes=[f"rn{r4}"])
            S_.op("dve", lambda e: e.tensor_scalar(out=rn, in0=rn, scalar1=1.0 / HD, scalar2=EPS, op0=ALU.mult, op1=ALU.add),
                  reads=[f"rn{r4}"], writes=[f"rn{r4}"])

        def s3():
            S_.op("act", lambda e: e.activation(out=rn, in_=rn, func=AF.Ln), reads=[f"rn{r4}"], writes=[f"rn{r4}"])
            S_.op("act", lambda e: e.activation(out=rn, in_=rn, func=AF.Exp, scale=-0.5), reads=[f"rn{r4}"], writes=[f"rn{r4}"])

        def s4():
            S_.op("dve", lambda e: e.tensor_tensor(out=dst_tile[:, :, 0:64], in0=ps_fn().rearrange("p (h d) -> p h d", h=nh),
                                                   in1=rn.unsqueeze(2).to_broadcast([128, nh, 64]), op=ALU.mult),
                  reads=[f"mm{st['m']}", f"rn{r4}"], writes=dst_res)

        return s1, s2, s3, s4

    item_ctr = [0]
    for q in range(NSEQ):
        for c in range(NCH):
            first_chunk = (q == 0 and c == 0)
            if first_chunk:
                for i in range(4):
                    r0 = q * S + (4 * c + i) * 128
                    S_.dma("sp", d_x[i], xh[:, i, :], x[r0:r0 + 128, :], writes=[f"xh{i}"])
            pipe = [norm_item(i, "gbuf", evac=("act" if i < 2 else "dve")) for i in range(4)] + [None, None]

            def i4_item(i):
                blk = 4 * c + i
                rb = blk % 8
                idx = item_ctr[0]
                item_ctr[0] += 1
                par = idx % 2
                st = {}
                fb = 64 + (i % 4) * 24
                z = small[:, fb:fb + 8]
                a_ = small[:, fb + 8:fb + 16]
                mn = small[:, fb + 16:fb + 24]
                fr = f"fg{i % 4}"

                def s0():
                    if i == 0:
                        st_sl["sl"] = ws_get(SL_I[4])
                    sl = st_sl["sl"]
                    m = next_mm()
                    st["m"] = m
                    ps = mmb[m]
                    S_.group("pe", [lambda e, kc=kc: e.matmul(ps[:, 0:264], xnT[:, kc, i * 128:(i + 1) * 128],
                                                               wslot[sl][:, kc * 512:kc * 512 + 264], start=(kc == 0), stop=(kc == KC - 1))
                                    for kc in range(KC)], reads=[f"xnT{i}", f"wslot{sl}"], writes=[f"mm{m}"])
                    if i == 3:
                        ws_done()

                r1, r2, r3, r4_ = rn_stages(st, lambda: mmb[st["m"]][:, 0:128], NKV, kAt[par], [f"kAt{par}"], idx)

                def s1():
                    r1()
                    ps = mmb[st["m"]]
                    S_.op("act", lambda e: e.activation(out=V_A[:, rb, :].rearrange("p (g c) -> p g c", c=65)[:, :, 0:64],
                                                        in_=ps[:, 128:256].rearrange("p (g d) -> p g d", g=NKV), func=AF.Copy),
                          reads=[f"mm{st['m']}", "V_A_ones"], writes=[f"V_A{rb}"])
                    S_.op("dve", lambda e: e.tensor_tensor(out=z, in0=ps[:, 256:264], in1=bfg[:], op=ALU.add), reads=[f"mm{st['m']}", "bfg"], writes=[fr + "z"])
                    S_.op("dve", lambda e: e.scalar_tensor_tensor(out=a_, in0=z, scalar=-1.0, in1=z, op0=ALU.mult, op1=ALU.max), reads=[fr + "z"], writes=[fr + "a"])
                    S_.op("dve", lambda e: e.tensor_single_scalar(out=mn, in_=z, scalar=0.0, op=ALU.min), reads=[fr + "z"], writes=[fr + "m"])

                def s2():
                    r2()
                    S_.op("act", lambda e: e.activation(out=a_, in_=a_, func=AF.Exp, scale=-1.0), reads=[fr + "a"], writes=[fr + "a"])
                    S_.op("act", lambda e: e.activation(out=a_, in_=a_, func=AF.Ln, bias=1.0), reads=[fr + "a"], writes=[fr + "a"])

                def s3():
                    r3()
                    S_.op("dve", lambda e: e.tensor_tensor(out=mn, in0=mn, in1=a_, op=ALU.subtract), reads=[fr + "m", fr + "a"], writes=[fr + "m"])

                def s4():
                    r4_()
                    fns = [lambda e: e.matmul(misc[:, 0:NHB], tri[:], mn, start=True, stop=(blk == 0))]
                    rds = ["tri", fr + "m"]
                    for j in range(i):
                        fbj = 64 + (j % 4) * 24
                        last = (j == i - 1) and (c == 0)
                        fns.append(lambda e, fbj=fbj, last=last: e.matmul(misc[:, 0:NHB], ones_f[:], small[:, fbj + 16:fbj + 24], start=False, stop=last))
                        rds += [f"fg{j % 4}m", "ones_f"]
                    if c > 0:
                        fns.append(lambda e: e.matmul(misc[:, 0:NHB], sel[:], cprev[:], start=False, stop=True))
                        rds += ["sel", "cprev"]
                    S_.group("pe", fns, reads=rds, writes=["misc"])
                    S_.op("dve", lambda e: e.tensor_copy(cchunk[:, i, :], misc[:, 0:NHB]), reads=["misc"], writes=[f"cc{i}"])
                    S_.op("dve", lambda e: e.tensor_scalar(out=negc[:, blk, :], in0=cchunk[:, i, :], scalar1=-1.0, scalar2=None, op0=ALU.mult),
                          reads=[f"cc{i}"], writes=[f"negc{blk}"])

                def s5():
                    t = next_tp()
                    st["t"] = t
                    tpv = tpb[t][:].rearrange("p (k n) -> p k n", k=KC)
                    S_.group("pe", [lambda e, g=g: e.transpose(tpv[0:66, g, :], kAt[par][:, g, :], ident_bf[:]) for g in range(NKV)],
                             reads=[f"kAt{par}", "ident_bf"], writes=[f"tp{t}"])

                def s6():
                    t = st["t"]
                    tpv = tpb[t][:].rearrange("p (k n) -> p k n", k=KC)
                    S_.op("act", lambda e: e.activation(out=KT_A[0:66, :, rb * 128:(rb + 1) * 128], in_=tpv[0:66, 0:NKV, :], func=AF.Copy),
                          reads=[f"tp{t}"], writes=[f"KT_A{rb}"])

                return [s0, s1, s2, s3, s4, s5, s6]

            def qk_item(slab_id, kind, i):
                blk = 4 * c + i
                idx = item_ctr[0]
                item_ctr[0] += 1
                par = idx % 2
                st = {}

                def s0():
                    if i == 0:
                        st_sl["sl"] = ws_get(slab_id)
                    sl = st_sl["sl"]
                    m = next_mm()
                    st["m"] = m
                    ps = mmb[m]
                    S_.group("pe", [lambda e, kc=kc: e.matmul(ps[:], xnT[:, kc, i * 128:(i + 1) * 128],
                                                               wslot[sl][:, kc * 512:(kc + 1) * 512], start=(kc == 0), stop=(kc == KC - 1))
                                    for kc in range(KC)], reads=[f"xnT{i}", f"wslot{sl}"], writes=[f"mm{m}"])
                    if i == 3:
                        ws_done()

                if kind == "vB":
                    def s1v():
                        ps = mmb[st["m"]]
                        S_.op("act", lambda e: e.activation(out=V_B[:, blk, :].rearrange("p (h c) -> p h c", c=65)[:, :, 0:64],
                                                            in_=ps[:].rearrange("p (h d) -> p h d", h=NHB), func=AF.Copy),
                              reads=[f"mm{st['m']}", "V_B_ones"], writes=[f"V_B{blk}"])
                    return [s0, s1v]
                tile_ = {"qA": qAt, "qB": qBt, "kB": kBt}[kind][par]
                tname = f"{kind}t{par}"
                nrow = 66 if kind == "qA" else 68
                s1, s2, s3, r4_ = rn_stages(st, lambda: mmb[st["m"]][:], 8, tile_, [tname], idx)

                def s4():
                    r4_()
                    if kind == "qB":
                        S_.op("dve", lambda e: e.tensor_copy(tile_[:, :, 64:65], cchunk[:, i, :].unsqueeze(2)), reads=[f"cc{i}"], writes=[tname])
                    if kind == "kB":
                        t1 = small[:, 160 + 16 * par:168 + 16 * par]
                        t2 = small[:, 168 + 16 * par:176 + 16 * par]
                        ng = negc[:, blk, :]
                        col = lambda k_: tile_[:, :, k_:k_ + 1].rearrange("p h o -> p (h o)")
                        S_.op("dve", lambda e: e.tensor_copy(col(65), ng), reads=[f"negc{blk}"], writes=[tname])
                        S_.op("dve", lambda e: e.tensor_tensor(out=t1, in0=ng, in1=col(65), op=ALU.subtract), reads=[f"negc{blk}", tname], writes=[f"spl{par}a"])
                        S_.op("dve", lambda e: e.tensor_copy(col(66), t1), reads=[f"spl{par}a"], writes=[tname])
                        S_.op("dve", lambda e: e.tensor_tensor(out=t2, in0=t1, in1=col(66), op=ALU.subtract), reads=[f"spl{par}a", tname], writes=[f"spl{par}b"])
                        S_.op("dve", lambda e: e.tensor_copy(col(67), t2), reads=[f"spl{par}b"], writes=[tname])

                def s5():
                    t = next_tp()
                    st["t"] = t
                    tpv = tpb[t][:].rearrange("p (k n) -> p k n", k=KC)
                    S_.group("pe", [lambda e, h=h: e.transpose(tpv[0:nrow, h, :], tile_[:, h, :], ident_bf[:]) for h in range(8)],
                             reads=[tname, "ident_bf"], writes=[f"tp{t}"])

                def s6():
                    t = st["t"]
                    tpv = tpb[t][:].rearrange("p (k n) -> p k n", k=KC)
                    if kind == "qA":
                        S_.op("act", lambda e: e.activation(out=QT_A[0:66, :, i, :].rearrange("p g (h n) -> p g h n", h=4),
                                                            in_=tpv[0:66, :, :].rearrange("p (g h) n -> p g h n", g=NKV),
                                                            func=AF.Copy, scale=scaleA[0:66, :]),
                              reads=[f"tp{t}", "scaleA"], writes=[f"QT_A{i}"])
                    elif kind == "qB":
                        S_.op("act", lambda e: e.activation(out=QT_B[0:68, :, i * 128:(i + 1) * 128], in_=tpv[0:68, :, :],
                                                            func=AF.Copy, scale=scaleB[0:68, :]),
                              reads=[f"tp{t}", "scaleB"], writes=[f"QT_B{i}"])
                    else:
                        S_.op("dve", lambda e: e.tensor_copy(KT_B[0:68, :, blk * 128:(blk + 1) * 128], tpv[0:68, :, :]),
                              reads=[f"tp{t}"], writes=[f"KT_B{blk}"])

                return [s0, s1, s2, s3, s4, s5, s6]

            st_sl = {}
            pipe += [i4_item(i) for i in range(4)]
            for slab_id, kind in ((SL_I[0], "qA"), (SL_I[2], "kB"), (SL_I[1], "qB"), (SL_I[3], "vB")):
                pipe += [qk_item(slab_id, kind, i) for i in range(4)]
            run_pipe(pipe, 7)
            S_.op("dve", lambda e: e.tensor_copy(cprev[:], cchunk[:, 3, :]), reads=["cc3"], writes=["cprev"])
            S_.dma("sp", d_g, gbuf[:], gM_d[:], writes=["gbuf"])

            items = []
            for h in range(NHB):
                NJ = 4 * c + 4
                for j in range(NJ):
                    items.append(dict(kind="B", h=h, j=j, first=(j == 0), last=(j == NJ - 1)))
            for i in range(4):
                blk = 4 * c + i
                for g in range(NKV):
                    pairs = ([(blk - 1, "P")] if blk > 0 else []) + [(blk, "C")]
                    for pi_, (kb, mk) in enumerate(pairs):
                        items.append(dict(kind="A", i=i, g=g, kb=kb, mk=mk, first=(pi_ == 0), last=(pi_ == len(pairs) - 1)))
            units = []
            for it in items:
                key = ("A", it["i"], it["g"]) if it["kind"] == "A" else ("B", it["h"])
                it["key"] = key
                if units and len(units[-1]) == 1 and units[-1][0]["key"] == key:
                    units[-1].append(it)
                else:
                    units.append([it])
            pend_A = []
            ag_i = [0]
            pend_fin = []
            oa_i = [0]
            un_i = [0]
            bc_i = [0]
            rd_i = [0]
            cur_oa = {}

            def acc(m):
                return (mmb[4], "mm4") if m == 4 else (tp1_f32, "tp1")

            def stage_S(unit):
                u = un_i[0] % 2
                u3 = un_i[0] % 3
                un_i[0] += 1
                widths = []
                for e_, it in enumerate(unit):
                    m = 2 * u + e_
                    it["pt"] = 2 * u3 + e_
                    ps = mmb[m]
                    if it["kind"] == "A":
                        i, g, kb = it["i"], it["g"], it["kb"]
                        rb = kb % 8
                        mask = maskP[:, g, :] if it["mk"] == "P" else maskC[:]
                        S_.group("pe", [lambda e: e.matmul(ps[:], KT_A[0:66, g, rb * 128:(rb + 1) * 128], QT_A[0:66, g, i, :], start=True, stop=False),
                                        lambda e: e.matmul(ps[:], ident_bf[:], mask, start=False, stop=True)],
                                 reads=[f"KT_A{rb}", f"QT_A{i}", "ident_bf", "maskP", "maskC"], writes=[f"mm{m}"])
                        it["N"] = 512
                        it["off"] = 0
                    else:
                        h, j = it["h"], it["j"]
                        jj = j - 4 * c
                        off = max(jj, 0) * 128
                        N = 512 - off
                        fns = [lambda e: e.matmul(ps[:, 0:N], KT_B[0:68, h, j * 128:(j + 1) * 128], QT_B[0:68, h, off:512], start=True, stop=(jj < 0))]
                        if jj >= 0:
                            fns.append(lambda e: e.matmul(ps[:, 0:128], ident_bf[:], maskC[:, 0:128], start=False, stop=True))
                        S_.group("pe", fns, reads=[f"KT_B{j}"] + [f"QT_B{ii}" for ii in range(4)] + ["ident_bf", "maskC"], writes=[f"mm{m}"])
                        it["N"] = N
                        it["off"] = off
                    widths.append(it["N"])
                if len(unit) == 2 and widths[0] == 512:
                    W = 512 + widths[1]
                    S_.op("act", lambda e: e.activation(out=PTp[u3][:, 0:W], in_=pairb[u][:, 0:W], func=AF.Exp),
                          reads=[f"mm{2 * u}", f"mm{2 * u + 1}"], writes=R_PT[2 * u3] + R_PT[2 * u3 + 1])
                else:
                    for e_, it in enumerate(unit):
                        k = 2 * u + e_
                        kp = 2 * u3 + e_
                        N = it["N"]
                        S_.op("act", lambda e, k=k, kp=kp, N=N: e.activation(out=PT[kp][:, 0:N], in_=mmb[k][:, 0:N], func=AF.Exp),
                              reads=[f"mm{k}"], writes=R_PT[kp])

            def stage_PV(it):
                key = it["key"]
                if it["first"]:
                    cur_oa[key] = (4, 5)[oa_i[0] % 2]
                    oa_i[0] += 1
                    for ent in [p for p in pend_fin if cur_oa[p[2]] == cur_oa[key] and p[2] != key]:
                        pend_fin.remove(ent)
                        stage_fin2(ent[1])
                m = cur_oa[key]
                oa, bank = acc(m)
                k = it["pt"]
                N, off = it["N"], it["off"]
                if it["kind"] == "A":
                    rb = it["kb"] % 8
                    g = it["g"]
                    S_.group("pe", [lambda e, hh=hh: e.matmul(oa[:, hh * 65:(hh + 1) * 65], PT[k][:, hh * 128:(hh + 1) * 128],
                                                               V_A[:, rb, g * 65:(g + 1) * 65], start=(it["first"] and hh == 0), stop=it["last"],
                                                               skip_group_check=True)
                                    for hh in range(4)],
                             reads=[f"V_A{rb}", "V_A_ones"] + R_PT[k], writes=[bank])
                else:
                    h, j = it["h"], it["j"]
                    S_.op("pe", lambda e: e.matmul(oa[0:65, off:512], V_B[:, j, h * 65:(h + 1) * 65], PT[k][:, 0:N], start=it["first"], stop=it["last"]),
                          reads=[f"V_B{j}", "V_B_ones"] + R_PT[k], writes=[bank])

            RDP = [0, 32, 64]

            def stage_finA1(it):
                key = ("A", it["i"], it["g"])
                oa, bank = acc(cur_oa[key])
                g = it["g"]
                r = ag_i[0] % 2
                ag_i[0] += 1
                it["ag"] = r
                den4 = small[:, 40 + 8 * r:44 + 8 * r]
                oav = oa[:, 0:260].rearrange("p (h c) -> p h c", c=65)
                S_.op("dve", lambda e: e.tensor_tensor(out=den4, in0=oav[:, :, 64], in1=sinkb[:, g * 4:(g + 1) * 4], op=ALU.add),
                      reads=[bank, "sinkb"], writes=[f"den4{r}"])
                S_.op("dve", lambda e: e.reciprocal(den4, den4), reads=[f"den4{r}"], writes=[f"den4{r}"])
                S_.op("dve", lambda e: e.tensor_tensor(out=mtok[r][:].rearrange("p (h d) -> p h d", h=4), in0=oav[:, :, 0:64],
                                                       in1=den4.unsqueeze(2).to_broadcast([128, 4, 64]), op=ALU.mult),
                      reads=[bank, f"den4{r}"], writes=R_mtok[r])

            def stage_finA2(it):
                i, g = it["i"], it["g"]
                r = it["ag"]
                t = 0
                S_.group("pe", [lambda e, a=a: e.transpose(tpb[t][:, a * 128:(a + 1) * 128], mtok[r][:, a * 128:(a + 1) * 128], ident_bf[:])
                                for a in range(2)], reads=R_mtok[r] + ["ident_bf"], writes=[f"tp{t}"])
                S_.op("dve", lambda e: e.tensor_copy(mixT[:, g * 2:g * 2 + 2, i * 128:(i + 1) * 128],
                                                     tpb[t][:, 0:256].rearrange("p (a n) -> p a n", a=2)),
                      reads=[f"tp{t}"], writes=[f"mixT{g * 2}", f"mixT{g * 2 + 1}"])

            def stage_fin1(it):
                key = ("B", it["h"])
                oa, bank = acc(cur_oa[key])
                r = rd_i[0] % 3
                rd_i[0] += 1
                it["rd"] = r
                p0 = RDP[r]
                rrow = rden[p0:p0 + 1, :]
                if 4 * c + 4 <= 8 or it["h"] == NHB - 1:
                    S_.op("act", lambda e: e.activation(out=rrow, in_=oa[64:65, :], func=AF.Ln), reads=[bank], writes=[f"BC|rden{r}|sh4"])
                    S_.op("act", lambda e: e.activation(out=rrow, in_=rrow, func=AF.Exp, scale=-1.0), reads=[f"BC|rden{r}|sh4"], writes=[f"BC|rden{r}|sh4"])
                else:
                    S_.op("dve", lambda e: e.reciprocal(rrow, oa[64:65, :]), reads=[bank], writes=[f"BC|rden{r}|sh4"])

            def stage_fin2(it):
                key = ("B", it["h"])
                oa, bank = acc(cur_oa[key])
                b = 0
                r = it["rd"]
                p0 = RDP[r]
                S_.op("pe", lambda e: e.matmul(misc[0:64, :], ones_f[p0:p0 + 1, 0:64], rden[p0:p0 + 1, :], start=True, stop=True),
                      reads=["ones_f", f"BC|rden{r}|sh4"], writes=["misc"])
                S_.op("dve", lambda e: e.tensor_copy(bc_sb[b][0:64, :], misc[0:64, :]), reads=["misc"], writes=R_bc[b])
                h = it["h"]
                par = h % 2
                ec = 4 + h // 2
                S_.op("dve", lambda e: e.tensor_tensor(out=mixT[par * 64:(par + 1) * 64, ec, :], in0=oa[0:64, :], in1=bc_sb[b][0:64, :], op=ALU.mult),
                      reads=[bank] + R_bc[b], writes=[f"mixT{ec}"])

            FDELAY = 4
            n_u = len(units)
            k = 0
            while k < n_u + 2 or pend_fin or pend_A:
                if k < n_u:
                    stage_S(units[k])
                if 0 <= k - 2 < n_u:
                    for it in units[k - 2]:
                        stage_PV(it)
                        if it["last"]:
                            if it["kind"] == "A":
                                while len(pend_A) > 1:
                                    stage_finA2(pend_A.pop(0)[1])
                                stage_finA1(it)
                                pend_A.append((k + 2, it))
                            else:
                                while len(pend_fin) > 2:
                                    stage_fin2(pend_fin.pop(0)[1])
                                stage_fin1(it)
                                pend_fin.append((k + FDELAY, it, it["key"]))
                while pend_fin and pend_fin[0][0] <= k:
                    stage_fin2(pend_fin.pop(0)[1])
                while pend_A and pend_A[0][0] <= k:
                    stage_finA2(pend_A.pop(0)[1])
                k += 1

            def op_item(dh, i):
                st = {}

                def s0():
                    if i == 0 and dh == 0:
                        st_sl["o"] = [ws_get(SL_O[0]), ws_get(SL_O[1], ahead=1)]
                    sl = st_sl["o"][dh]
                    m = next_mm()
                    st["m"] = m
                    ps = mmb[m]
                    S_.group("pe", [lambda e, ec=ec: e.matmul(ps[:], mixT[:, ec, i * 128:(i + 1) * 128], wslot[sl][:, ec * 512:(ec + 1) * 512],
                                                               start=(ec == 0), stop=(ec == KC - 1)) for ec in range(KC)],
                             reads=[f"mixT{ec}" for ec in range(KC)] + [f"wslot{sl}"], writes=[f"mm{m}"])
                    if i == 3 and dh == 1:
                        ws_done()
                        ws_done()

                def s1():
                    ps = mmb[st["m"]]
                    S_.op("dve", lambda e: e.tensor_tensor(out=xh[:, i, dh * 512:(dh + 1) * 512], in0=ps[:], in1=xh[:, i, dh * 512:(dh + 1) * 512], op=ALU.add),
                          reads=[f"mm{st['m']}", f"xh{i}"], writes=[f"xh{i}"])

                return [s0, s1]

            o = {(i, dh): op_item(dh, i) for i in range(4) for dh in range(2)}
            n_ = [norm_item(i, "gbuf", evac="act") for i in range(4)]
            pipe = [o[0, 0], o[0, 1], o[1, 0], n_[0], o[1, 1], o[2, 0], n_[1], o[2, 1], o[3, 0], n_[2], o[3, 1], None, n_[3]]
            run_pipe(pipe, 6)
            last_chunk = (q == NSEQ - 1 and c == NCH - 1)
            if not last_chunk:
                S_.dma("sp", d_g, gbuf[:], gA_d[:], writes=["gbuf"])

            def ffn_up(g):
                sl = ws_get(SL_U[g])
                hb = g % 2
                ms = [next_mm() for _ in range(4)]
                parts = [(0, 384, [0, 1, 2]), (384, 512, [3])] if g == 0 else [(0, 512, [0, 1, 2, 3])]
                for c0, c1, tiles in parts:
                    for fc in range(4):
                        ps = mmb[ms[fc]]
                        S_.group("pe", [lambda e, kc=kc: e.matmul(ps[:, c0:c1], wslot[sl][:, kc * 512 + fc * 128:kc * 512 + (fc + 1) * 128],
                                                                   xnT[:, kc, c0:c1], start=(kc == 0), stop=(kc == KC - 1)) for kc in range(KC)],
                                 reads=[f"xnT{i}" for i in tiles] + [f"wslot{sl}"], writes=[f"mm{ms[fc]}"])
                for fc in range(4):
                    m = ms[fc]
                    ps = mmb[m]
                    rb_ = fc % 2
                    S_.op("act", lambda e: e.activation(out=rtmp[rb_][:], in_=ps[:], func=AF.Relu), reads=[f"mm{m}"], writes=R_rtmp[rb_])
                    S_.op("pool", lambda e: e.tensor_tensor(out=hid[hb][:, fc, :], in0=rtmp[rb_][:], in1=rtmp[rb_][:], op=ALU.mult),
                          reads=R_rtmp[rb_], writes=R_hid[hb])
                ws_done()

            def ffn_down(g):
                sl = ws_get(SL_D[g])
                hb = g % 2
                nq, ncn = (q, c + 1) if c + 1 < NCH else (q + 1, 0)
                for i in range(4):
                    for dh in range(2):
                        m = next_mm()
                        ps = mmb[m]
                        S_.group("pe", [lambda e, fc=fc: e.matmul(ps[:], hid[hb][:, fc, i * 128:(i + 1) * 128],
                                                                   wslot[sl][:, fc * 1024 + dh * 512:fc * 1024 + (dh + 1) * 512],
                                                                   start=(fc == 0), stop=(fc == 3)) for fc in range(4)],
                                 reads=R_hid[hb] + [f"wslot{sl}"], writes=[f"mm{m}"])
                        if g < NG - 1:
                            S_.op("dve", lambda e: e.tensor_tensor(out=xh[:, i, dh * 512:(dh + 1) * 512], in0=ps[:], in1=xh[:, i, dh * 512:(dh + 1) * 512], op=ALU.add),
                                  reads=[f"mm{m}", f"xh{i}"], writes=[f"xh{i}"])
                        else:
                            yb = (2 * i + dh) % 4
                            r0 = q * S + (4 * c + i) * 128
                            S_.op("dve", lambda e: e.tensor_tensor(out=yout[yb][:], in0=ps[:], in1=xh[:, i, dh * 512:(dh + 1) * 512], op=ALU.add),
                                  reads=[f"mm{m}", f"xh{i}"], writes=R_yout[yb])
                            S_.dma("sp", d_yb[yb], y[r0:r0 + 128, dh * 512:(dh + 1) * 512], yout[yb][:], reads=R_yout[yb], writes=[f"y{yb}"])
                    if g == NG - 1 and nq < NSEQ:
                        r1 = nq * S + (4 * ncn + i) * 128
                        S_.dma("sp", d_x[i], xh[:, i, :], x[r1:r1 + 128, :], writes=[f"xh{i}"])
                ws_done()

            for g in range(NG + 1):
                if g < NG:
                    ffn_up(g)
                if g >= 1:
                    ffn_down(g - 1)

    S_.wait_all("sp")
    return nc, S_


_PROG_CACHE = {}


def make_in_maps(x2d_list, w_in, w_out, w_up, w_down, attn_norm_g, mlp_norm_g, b_forget, q_norm_a, k_norm_a,
                 sink_logits, q_norm_b, k_norm_b):
    c = host_consts()
    f = lambda a: np.ascontiguousarray(np.asarray(a, dtype=np.float32))
    common = {
        "w_in": f(w_in), "w_out": f(w_out), "w_up": f(w_up), "w_down": f(w_down),
        "gA": f(np.broadcast_to(np.asarray(attn_norm_g)[None, :], (128, D))),
        "gM": f(np.broadcast_to(np.asarray(mlp_norm_g)[None, :], (128, D))),
        "bfg": f(np.broadcast_to(np.asarray(b_forget)[None, :], (128, NHB))),
        "gains": f(np.stack([np.asarray(q_norm_a), np.asarray(k_norm_a), np.asarray(q_norm_b), np.asarray(k_norm_b)], axis=1)),
        "sinks": f(np.broadcast_to(np.asarray(sink_logits)[None, :], (128, NHA))),
        **{k: f(v) for k, v in c.items()},
    }
    return [{"x": f(xs), **common} for xs in x2d_list]


def kernel(x, attn_norm_g, w_in, b_forget, q_norm_a, k_norm_a, sink_logits, q_norm_b, k_norm_b, w_out,
           mlp_norm_g, w_up, w_down):
    x = np.asarray(x)
    B, S, _ = x.shape
    DFF = np.asarray(w_up).shape[1]
    n = 8
    per = B // n
    key = (per, S, DFF)
    if key not in _PROG_CACHE:
        _PROG_CACHE[key] = build_program(per, S, DFF)[0]
    nc = _PROG_CACHE[key]
    shards = [x[i * per:(i + 1) * per].reshape(per * S, D) for i in range(n)]
    in_maps = make_in_maps(shards, w_in, w_out, w_up, w_down, attn_norm_g, mlp_norm_g, b_forget, q_norm_a, k_norm_a,
                           sink_logits, q_norm_b, k_norm_b)
    res = run_bass_kernel_spmd(nc, in_maps, core_ids=list(range(n)))
    out = np.concatenate([np.asarray(r["y"]).reshape(per, S, D) for r in res.results], axis=0)
    return out.astype(np.float32)
```

```python
import numpy as np
import concourse.bass as bass
import concourse.mybir as mybir
from concourse.bass_utils import run_bass_kernel_spmd

F32 = mybir.dt.float32
BF16 = mybir.dt.bfloat16
ALU = mybir.AluOpType
AF = mybir.ActivationFunctionType
AX = mybir.AxisListType

SEM_ROLL = 30000
D = 1024
KC = 8
HD = 64
NHA = 8
NKV = 2
NHB = 8
INW = 2312
EPS = 1e-6
NEG = -30000.0
NSLOT = 3


class DmaSem:
    def __init__(self, nc, name):
        self.nc = nc
        self.name = name
        self.sem = nc.alloc_semaphore(name)
        self.count = 0


class Sched:
    def __init__(self, nc):
        self.nc = nc
        self.eng = {"pe": nc.tensor, "act": nc.scalar, "dve": nc.vector, "pool": nc.gpsimd, "sp": nc.sync}
        self.sem = {}
        self.cnt = {}
        self.nsem = 0
        for e in ("pe", "act", "dve", "pool"):
            self._new_sem(e)
        self.waited = {e: {} for e in self.eng}
        self.res = {}
        self.n_ops = 0
        self.sh_owner = {}

    def _expand(self, reads, writes):
        r2, w2 = [], []
        for lst, is_w in ((reads, False), (writes, True)):
            for name in lst:
                if "|" in name:
                    fam, fine, gr = name.split("|")
                    (w2 if is_w else r2).append(fine)
                    for g in gr.split(","):
                        if self.sh_owner.get(g) != fam:
                            self.sh_owner[g] = fam
                            w2.append(g)
                        else:
                            r2.append(g)
                else:
                    (w2 if is_w else r2).append(name)
        return r2, w2

    def _new_sem(self, e):
        self.sem[e] = self.nc.alloc_semaphore(f"s_{e}_{self.nsem}")
        self.nsem += 1
        self.cnt[e] = 0

    def _wait(self, e, tok):
        if tok is None:
            return
        sem, val, src = tok
        if src == e and e == "pe":
            return
        w = self.waited[e]
        if w.get(sem.name, 0) >= val:
            return
        w[sem.name] = val
        self.eng[e].wait_ge(sem, val)

    @staticmethod
    def _is_psum(r):
        return r.startswith("mm") or r.startswith("tp") or r == "misc"

    def deps(self, e, reads, writes):
        for r in reads:
            st = self.res.get(r)
            if st:
                self._wait(e, st["w"])
                if self._is_psum(r):
                    for t in st["r"].values():
                        if t[2] != e:
                            self._wait(e, t)
        for r in writes:
            st = self.res.get(r)
            if st:
                self._wait(e, st["w"])
                for t in st["r"].values():
                    self._wait(e, t)

    def commit(self, tok, reads, writes):
        for r in reads:
            st = self.res.setdefault(r, {"w": None, "r": {}})
            st["r"][tok[0].name] = tok
        for r in writes:
            self.res[r] = {"w": tok, "r": {}}

    def _signal(self, e, ins, reads, writes):
        if self.cnt[e] >= SEM_ROLL:
            self._new_sem(e)
        self.cnt[e] += 1
        ins.then_inc(self.sem[e], 1)
        tok = (self.sem[e], self.cnt[e], e)
        self.commit(tok, reads, writes)
        self.n_ops += 1
        return tok

    def op(self, e, fn, reads=(), writes=()):
        reads, writes = self._expand(reads, writes)
        self.deps(e, reads, writes)
        return self._signal(e, fn(self.eng[e]), reads, writes)

    def group(self, e, fns, reads=(), writes=()):
        reads, writes = self._expand(reads, writes)
        self.deps(e, reads, writes)
        ins = None
        for fn in fns:
            ins = fn(self.eng[e])
        return self._signal(e, ins, reads, writes)

    def dma(self, q, dsem, out, in_, reads=(), writes=(), **kw):
        reads, writes = self._expand(reads, writes)
        self.deps(q, reads, writes)
        ins = self.eng[q].dma_start(out=out, in_=in_, **kw)
        if dsem.count >= SEM_ROLL:
            dsem.sem = self.nc.alloc_semaphore(f"{dsem.name}_r{self.nsem}")
            self.nsem += 1
            dsem.count = 0
        dsem.count += 16
        ins.then_inc(dsem.sem, 16)
        tok = (dsem.sem, dsem.count, "dma")
        self.commit(tok, reads, writes)
        return tok

    def wait_all(self, e):
        best = {}
        for st in self.res.values():
            for t in [st["w"], *st["r"].values()]:
                if t is not None and (t[0].name not in best or best[t[0].name][1] < t[1]):
                    best[t[0].name] = t
        for t in best.values():
            self._wait(e, t)


def alibi_slopes():
    return [2.0 ** (-(h + 1)) for h in range(NHA)]


def host_consts():
    p = np.arange(128, dtype=np.float32)
    c = {}
    c["ident"] = np.eye(128, dtype=np.float32)
    c["tri"] = (p[:, None] <= p[None, :]).astype(np.float32)
    sel = np.zeros((128, 128), np.float32)
    sel[127, :] = 1.0
    c["sel"] = sel
    mc = np.where(p[:, None] <= p[None, :], 0.0, NEG).astype(np.float32)
    c["maskC"] = np.tile(mc, (1, 4))
    sl = alibi_slopes()
    mp = np.zeros((NKV, 128, 512), np.float32)
    for g in range(NKV):
        for hh in range(4):
            s_ = sl[g * 4 + hh]
            mp[g, :, hh * 128:(hh + 1) * 128] = np.where(p[:, None] > p[None, :], -128.0 * s_, NEG)
    c["maskP"] = mp
    aq = np.zeros((128, NHA, 2), np.float32)
    for h in range(NHA):
        aq[:, h, 0] = -sl[h] * p
        aq[:, h, 1] = sl[h]
    c["augqA"] = aq
    ak = np.zeros((128, 2), np.float32)
    ak[:, 0] = 1.0
    ak[:, 1] = p
    c["augkA"] = ak
    return c


def build_program(NSEQ, S, DFF):
    NB = S // 128
    NCH = S // 512
    NG = DFF // 512
    NTOK = NSEQ * S
    nc = bass.Bass("TRN2", target_bir_lowering=False)
    S_ = Sched(nc)

    def dram_in(name, shape, dt=F32):
        return nc.dram_tensor(name, shape, dt, kind="ExternalInput").ap()

    x = dram_in("x", [NTOK, D])
    w_in = dram_in("w_in", [D, INW])
    w_out = dram_in("w_out", [D, D])
    w_up = dram_in("w_up", [D, DFF])
    w_down = dram_in("w_down", [DFF, D])
    gA_d = dram_in("gA", [128, D])
    gM_d = dram_in("gM", [128, D])
    bfg_d = dram_in("bfg", [128, NHB])
    gains_d = dram_in("gains", [64, 4])
    sinks_d = dram_in("sinks", [128, NHA])
    ident_d = dram_in("ident", [128, 128])
    tri_d = dram_in("tri", [128, 128])
    sel_d = dram_in("sel", [128, 128])
    maskC_d = dram_in("maskC", [128, 512])
    maskP_d = dram_in("maskP", [NKV, 128, 512])
    augqA_d = dram_in("augqA", [128, NHA, 2])
    augkA_d = dram_in("augkA", [128, 2])
    y = nc.dram_tensor("y", [NTOK, D], F32, kind="ExternalOutput").ap()

    NSLAB = 5 + 2 + 2 * NG
    wscr = nc.dram_tensor("wscr", [NSLAB, 128, 4096], BF16).ap()
    SL_I = [0, 1, 2, 3, 4]
    SL_O = [5, 6]
    SL_U = [7 + 2 * g for g in range(NG)]
    SL_D = [8 + 2 * g for g in range(NG)]

    def sb(name, shape, dt):
        return nc.alloc_sbuf_tensor("sb_" + name, shape, dt)

    wslot = [sb(f"wslot{k}", [128, 4096], BF16) for k in range(NSLOT)]
    KT_B = sb("KT_B", [128, NHB, S], BF16)
    V_B = sb("V_B", [128, NB, NHB * 65], BF16)
    KT_A = sb("KT_A", [128, NKV, 8 * 128], BF16)
    V_A = sb("V_A", [128, 8, NKV * 65], BF16)
    xh = sb("xh", [128, 4, D], F32)
    xnT = sb("xnT", [128, KC, 512], BF16)
    QT_B = sb("QT_B", [128, NHB, 512], BF16)
    QT_A = sb("QT_A", [128, NKV, 4, 4 * 128], BF16)
    mixT = sb("mixT", [128, KC, 512], BF16)
    gbuf = sb("gbuf", [128, D], F32)
    ident_bf = sb("ident_bf", [128, 128], BF16)
    tri = sb("tri", [128, 128], F32)
    sel = sb("sel", [128, 128], F32)
    maskC = sb("maskC", [128, 512], BF16)
    maskP = sb("maskP", [128, NKV, 512], BF16)
    ones_f = sb("ones_f", [128, 128], F32)
    negc = sb("negc", [128, NB, NHB], F32)
    cchunk = sb("cchunk", [128, 4, NHB], F32)
    bfg = sb("bfg", [128, NHB], F32)
    gains = sb("gains", [64, 4], F32)
    scaleA = sb("scaleA", [128, 1], F32)
    scaleB = sb("scaleB", [128, 1], F32)
    sinkb = sb("sinkb", [128, NHA], F32)
    small = sb("small", [128, 192], F32)
    cprev = sb("cprev", [128, NHB], F32)
    shared = sb("shared", [128, 19456], mybir.dt.uint8)

    def view(off, shape, dt):
        nbytes = int(np.prod(shape[1:])) * (4 if dt == F32 else 2)
        ap = shared[:, off:off + nbytes].bitcast(dt)
        if len(shape) == 3:
            ap = ap.rearrange("p (a b) -> p a b", a=shape[1])
        return ap

    xpre = view(0, [128, 1024], F32)
    R_xpre = ["A|xpre|sh0,sh1"]
    stage = [view(0, [128, 512], F32), view(2048, [128, 512], F32)]
    sqb = [view(4096, [128, 512], F32), view(6144, [128, 512], F32)]
    junk = view(4096, [128, 1024], F32)
    xnb = [view(8192, [128, 1024], BF16), view(10240, [128, 1024], BF16)]
    qAt = [view(12288, [128, NHA, 66], BF16), view(12288 + 1056, [128, NHA, 66], BF16)]
    qBt = [view(14400, [128, NHB, 68], BF16), view(14400 + 1088, [128, NHB, 68], BF16)]
    kBt = [view(16576, [128, NHB, 68], BF16), view(16576 + 1088, [128, NHB, 68], BF16)]
    kAt = [view(18752, [128, NKV, 66], BF16), view(18752 + 264, [128, NKV, 66], BF16)]
    PT = [view(1024 * k, [128, 512], BF16) for k in range(6)]
    PTp = [view(2048 * k, [128, 1024], BF16) for k in range(3)]
    rden = view(8192, [128, 512], F32)
    bc_sb = [view(6144, [128, 512], F32)]
    mtok = [view(10240, [128, 256], BF16), view(10240 + 512, [128, 256], BF16)]
    yout = [view(off, [128, 512], F32) for off in (0, 2048, 8192, 10240)]
    R_yout = [["Y|yout0|sh0"], ["Y|yout1|sh1"], ["Y|yout2|sh4"], ["Y|yout3|sh5"]]
    hid = [view(0, [128, 4, 512], BF16), view(4096, [128, 4, 512], BF16)]
    rtmp = [view(8192, [128, 512], F32), view(10240, [128, 512], F32)]
    R_stage = [["A|stage0|sh0"], ["A|stage1|sh1"]]
    R_sqb = [["A|sqb0|sh2"], ["A|sqb1|sh3"]]
    R_junk = R_sqb[0] + R_sqb[1]
    R_xnb = [["A|xnb0|sh4"], ["A|xnb1|sh5"]]
    R_PT = [[f"BC|PT{k}|sh{k // 2}"] for k in range(6)]
    R_bc = [["BC|bc0|sh3"]]
    R_mtok = [["BC|mtok0|sh5"], ["BC|mtok1|sh5"]]
    R_hid = [["E|hid0|sh0,sh1"], ["E|hid1|sh2,sh3"]]
    R_rtmp = [["E|rtmp0|sh4"], ["E|rtmp1|sh5"]]

    pairb = [nc.alloc_psum_tensor(f"pair{k}", [128, 1024], F32) for k in range(2)]
    mm4 = nc.alloc_psum_tensor("mm4", [128, 512], F32)
    mmb = [pairb[0][:, 0:512], pairb[0][:, 512:1024], pairb[1][:, 0:512], pairb[1][:, 512:1024], mm4[:]]
    tpb = [nc.alloc_psum_tensor(f"tp{k}", [128, 1024], BF16) for k in range(2)]
    misc = nc.alloc_psum_tensor("misc", [128, 512], F32)
    tp1_f32 = tpb[1][:].bitcast(F32)
    mm_i = [0]

    def next_mm():
        k = mm_i[0] % 5
        mm_i[0] += 1
        return k

    tp_i = [0]

    def next_tp():
        k = tp_i[0] % 2
        tp_i[0] += 1
        return k

    d_const = DmaSem(nc, "d_const")
    d_constp = DmaSem(nc, "d_constp")
    d_aug = DmaSem(nc, "d_aug")
    d_scr = [DmaSem(nc, f"d_scr{k}") for k in range(NSLAB)]
    d_slot = [DmaSem(nc, f"d_slot{k}") for k in range(NSLOT)]
    d_x = [DmaSem(nc, f"d_x{k}") for k in range(4)]
    d_xp = [DmaSem(nc, f"d_xp{k}") for k in range(4)]
    d_yb = [DmaSem(nc, f"d_yb{k}") for k in range(4)]
    d_y = [DmaSem(nc, f"d_y{k}") for k in range(4)]
    d_g = DmaSem(nc, "d_g")

    cst_tmp = view(0, [128, 512], F32)
    cst_tmp2 = view(2048, [128, 1024], F32)

    def load_const(dst, src, name):
        S_.dma("sp", d_const, dst, src, writes=[name])

    load_const(tri[:], tri_d[:], "tri")
    load_const(sel[:], sel_d[:], "sel")
    load_const(bfg[:], bfg_d[:], "bfg")
    load_const(gains[:], gains_d[:], "gains")
    load_const(sinkb[:], sinks_d[:], "sinkb")
    load_const(gbuf[:], gA_d[:], "gbuf")
    S_.dma("pool", d_constp, ident_bf[:], ident_d[:], writes=["ident_bf"])
    S_.dma("pool", d_constp, maskC[:], maskC_d[:], writes=["maskC"])
    S_.dma("pool", d_constp, maskP[:], maskP_d.rearrange("g p n -> p g n"), writes=["maskP"])
    tok_const = (d_const.sem, d_const.count, "dma")
    for r in ["tri", "sel", "bfg", "gains", "sinkb", "gbuf"]:
        S_.res[r] = {"w": tok_const, "r": {}}
    tok_constp = (d_constp.sem, d_constp.count, "dma")
    for r in ["ident_bf", "maskC", "maskP"]:
        S_.res[r] = {"w": tok_constp, "r": {}}

    def load_aug_consts():
        for k in range(2):
            S_.dma("pool", d_aug, kAt[k][:, :, 64:66], augkA_d.unsqueeze(1).to_broadcast([128, NKV, 2]), writes=[f"kAt{k}"])
        for k in range(2):
            S_.dma("pool", d_aug, qAt[k][:, :, 64:66], augqA_d[:], writes=[f"qAt{k}"])
        tok_aug = (d_aug.sem, d_aug.count, "dma")
        for r in ["qAt0", "qAt1", "kAt0", "kAt1"]:
            S_.res[r] = {"w": tok_aug, "r": {}}

    def scr3(k, a):
        return wscr[k].rearrange("p (a b) -> p a b", a=a)

    w_in_v = w_in.rearrange("(kc p) n -> p kc n", p=128)
    w_out_v = w_out.rearrange("(kc p) n -> p kc n", p=128)
    w_up_v = w_up.rearrange("(kc p) n -> p kc n", p=128)
    w_down_v = w_down.rearrange("(fc p) n -> p fc n", p=128)

    def conv(k, dst, src):
        S_.dma("pool", d_scr[k], dst, src, writes=[f"scr{k}"])

    conv(4, scr3(4, KC)[:, :, 0:256], w_in_v[:, :, 512:768])
    conv(4, scr3(4, KC)[:, :, 256:264], w_in_v[:, :, 2304:2312])
    S_.res["scr4"]["w"] = (d_scr[4].sem, d_scr[4].count, "dma")
    load_aug_consts()
    conv(0, scr3(0, KC), w_in_v[:, :, 0:512])
    conv(2, scr3(2, KC), w_in_v[:, :, 1280:1792])
    conv(1, scr3(1, KC), w_in_v[:, :, 768:1280])
    conv(3, scr3(3, KC), w_in_v[:, :, 1792:2304])
    for dh in range(2):
        conv(SL_O[dh], scr3(SL_O[dh], KC), w_out_v[:, :, dh * 512:(dh + 1) * 512])
    for g in range(NG):
        conv(SL_U[g], scr3(SL_U[g], KC), w_up_v[:, :, g * 512:(g + 1) * 512])
        conv(SL_D[g], scr3(SL_D[g], 4), w_down_v[:, g * 4:(g + 1) * 4, :])

    S_.op("pool", lambda e: e.memset(ones_f[:], 1.0), writes=["ones_f"])
    S_.op("pool", lambda e: e.memset(scaleA[:], 1.0), writes=["scaleA"])
    S_.op("pool", lambda e: e.memset(scaleB[:], 1.0), writes=["scaleB"])
    S_.op("pool", lambda e: e.memset(V_B[:].rearrange("p n (h c) -> p (n h) c", c=65)[:, :, 64:65], 1.0), writes=["V_B_ones"])
    S_.op("pool", lambda e: e.memset(V_A[:].rearrange("p n (h c) -> p (n h) c", c=65)[:, :, 64:65], 1.0), writes=["V_A_ones"])
    for k in range(2):
        S_.op("pool", lambda e, k=k: e.memset(kBt[k][:, :, 64:65], 1.0), writes=[f"kBt{k}"])
        S_.op("pool", lambda e, k=k: e.memset(qBt[k][:, :, 65:68], 1.0), writes=[f"qBt{k}"])
    S_.op("dve", lambda e: e.scalar_tensor_tensor(out=scaleA[0:64, :], in0=gains[:, 0:1], scalar=0.125, in1=gains[:, 1:2],
                                                   op0=ALU.mult, op1=ALU.mult), reads=["gains"], writes=["scaleA"])
    S_.op("dve", lambda e: e.scalar_tensor_tensor(out=scaleB[0:64, :], in0=gains[:, 2:3], scalar=0.125, in1=gains[:, 3:4],
                                                   op0=ALU.mult, op1=ALU.mult), reads=["gains"], writes=["scaleB"])
    S_.op("act", lambda e: e.activation(out=sinkb[:], in_=sinkb[:], func=AF.Exp), reads=["sinkb"], writes=["sinkb"])

    wseq = []
    for _q in range(NSEQ):
        for _c in range(NCH):
            wseq += [SL_I[4], SL_I[0], SL_I[2], SL_I[1], SL_I[3]]
            wseq += SL_O
            order = []
            for g in range(NG + 1):
                if g < NG:
                    order.append(SL_U[g])
                if g >= 1:
                    order.append(SL_D[g - 1])
            wseq += order
    ws = {"pos": 0, "loaded": 0}

    def ws_load(i):
        s = i % NSLOT
        if wseq[i] == SL_I[4]:
            S_.dma("sp", d_slot[s], wslot[s][:].rearrange("p (a b) -> p a b", a=KC)[:, :, 0:264], scr3(wseq[i], KC)[:, :, 0:264],
                   reads=[f"scr{wseq[i]}"], writes=[f"wslot{s}"])
        else:
            S_.dma("sp", d_slot[s], wslot[s][:], wscr[wseq[i]], reads=[f"scr{wseq[i]}"], writes=[f"wslot{s}"])

    def ws_get(expect, ahead=0):
        i = ws["pos"] + ahead
        assert wseq[i] == expect, (i, wseq[i], expect)
        while ws["loaded"] <= i:
            ws_load(ws["loaded"])
            ws["loaded"] += 1
        return i % NSLOT

    def ws_done():
        ws["pos"] += 1
        nxt = ws["pos"] - 1 + NSLOT
        if nxt < len(wseq) and ws["loaded"] == nxt:
            ws_load(nxt)
            ws["loaded"] += 1

    for i in range(min(NSLOT, len(wseq))):
        ws_load(i)
        ws["loaded"] += 1

    def run_pipe(items, nst):
        n = len(items)
        for step in range(n + nst - 1):
            for s in reversed(range(nst)):
                t = step - s
                if 0 <= t < n and items[t] is not None and s < len(items[t]) and items[t][s] is not None:
                    items[t][s]()

    def norm_item(i, gname, evac="dve", pre=False):
        xi = xpre[:] if pre else xh[:, i, :]
        xres = R_xpre if pre else [f"xh{i}"]
        b = i % 2
        ssc = small[:, 4 + i:5 + i]
        rsc = small[:, i:i + 1]
        st = {}

        def s0():
            S_.op("act", lambda e: e.activation(out=junk[:], in_=xi, func=AF.Square, accum_out=ssc),
                  reads=xres, writes=R_junk + [f"ss{i}"])
            if pre:
                S_.op("dve", lambda e: e.tensor_copy(xh[:, i, :], xi), reads=xres, writes=[f"xh{i}"])

        def s1():
            S_.op("dve", lambda e: e.tensor_scalar(out=rsc, in0=ssc, scalar1=1.0 / D, scalar2=EPS, op0=ALU.mult, op1=ALU.add),
                  reads=[f"ss{i}"], writes=[f"rs{i}"])

        def s2():
            S_.op("act", lambda e: e.activation(out=rsc, in_=rsc, func=AF.Ln), reads=[f"rs{i}"], writes=[f"rs{i}"])
            S_.op("act", lambda e: e.activation(out=rsc, in_=rsc, func=AF.Exp, scale=-0.5), reads=[f"rs{i}"], writes=[f"rs{i}"])

        def s3():
            S_.op("dve", lambda e: e.scalar_tensor_tensor(out=xnb[b][:], in0=xi, scalar=rsc, in1=gbuf[:], op0=ALU.mult, op1=ALU.mult),
                  reads=xres + [f"rs{i}", gname], writes=R_xnb[b])

        def s4():
            t = next_tp()
            st["t"] = t
            S_.group("pe", [lambda e, kc=kc: e.transpose(tpb[t][:, kc * 128:(kc + 1) * 128], xnb[b][:, kc * 128:(kc + 1) * 128], ident_bf[:])
                            for kc in range(KC)], reads=R_xnb[b] + ["ident_bf"], writes=[f"tp{t}"])

        def s5():
            t = st["t"]
            if evac == "act":
                S_.op("act", lambda e: e.activation(out=xnT[:, :, i * 128:(i + 1) * 128], in_=tpb[t][:].rearrange("p (k n) -> p k n", k=KC), func=AF.Copy),
                      reads=[f"tp{t}"], writes=[f"xnT{i}"])
            else:
                S_.op("dve", lambda e: e.tensor_copy(xnT[:, :, i * 128:(i + 1) * 128], tpb[t][:].rearrange("p (k n) -> p k n", k=KC)),
                      reads=[f"tp{t}"], writes=[f"xnT{i}"])

        return [s0, s1, s2, s3, s4, s5]

    def rn_stages(st, ps_fn, nh, dst_tile, dst_res, idx):
        par = idx % 2
        r4 = idx % 4
        w = nh * 64
        rn = small[:, 8 + 8 * r4:8 + 8 * r4 + nh]

        def s1():
            S_.op("act", lambda e: e.activation(out=sqb[par][:, 0:w], in_=ps_fn(), func=AF.Square), reads=[f"mm{st['m']}"], writes=R_sqb[par])

        def s2():
            S_.op("dve", lambda e: e.tensor_reduce(out=rn, in_=sqb[par][:, 0:w].rearrange("p (h d) -> p h d", h=nh), axis=AX.X, op=ALU.add),
                  reads=R_sqb[par], writes=[f"rn{r4}"])
            S_.op("dve", lambda e: e.tensor_scalar(out=rn, in0=rn, scalar1=1.0 / HD, scalar2=EPS, op0=ALU.mult, op1=ALU.add),
                  reads=[f"rn{r4}"], writes=[f"rn{r4}"])

        def s3():
            S_.op("act", lambda e: e.activation(out=rn, in_=rn, func=AF.Ln), reads=[f"rn{r4}"], writes=[f"rn{r4}"])
            S_.op("act", lambda e: e.activation(out=rn, in_=rn, func=AF.Exp, scale=-0.5), reads=[f"rn{r4}"], writes=[f"rn{r4}"])

        def s4():
            S_.op("dve", lambda e: e.tensor_tensor(out=dst_tile[:, :, 0:64], in0=ps_fn().rearrange("p (h d) -> p h d", h=nh),
                                                   in1=rn.unsqueeze(2).to_broadcast([128, nh, 64]), op=ALU.mult),
                  reads=[f"mm{st['m']}", f"rn{r4}"], writes=dst_res)

        return s1, s2, s3, s4

    item_ctr = [0]
    for q in range(NSEQ):
        for c in range(NCH):
            first_chunk = (q == 0 and c == 0)
            if first_chunk:
                for i in range(4):
                    r0 = q * S + (4 * c + i) * 128
                    S_.dma("sp", d_x[i], xh[:, i, :], x[r0:r0 + 128, :], writes=[f"xh{i}"])
            pipe = [norm_item(i, "gbuf", evac=("act" if i < 2 else "dve")) for i in range(4)] + [None, None]

            def i4_item(i):
                blk = 4 * c + i
                rb = blk % 8
                idx = item_ctr[0]
                item_ctr[0] += 1
                par = idx % 2
                st = {}
                fb = 64 + (i % 4) * 24
                z = small[:, fb:fb + 8]
                a_ = small[:, fb + 8:fb + 16]
                mn = small[:, fb + 16:fb + 24]
                fr = f"fg{i % 4}"

                def s0():
                    if i == 0:
                        st_sl["sl"] = ws_get(SL_I[4])
                    sl = st_sl["sl"]
                    m = next_mm()
                    st["m"] = m
                    ps = mmb[m]
                    S_.group("pe", [lambda e, kc=kc: e.matmul(ps[:, 0:264], xnT[:, kc, i * 128:(i + 1) * 128],
                                                               wslot[sl][:, kc * 512:kc * 512 + 264], start=(kc == 0), stop=(kc == KC - 1))
                                    for kc in range(KC)], reads=[f"xnT{i}", f"wslot{sl}"], writes=[f"mm{m}"])
                    if i == 3:
                        ws_done()

                r1, r2, r3, r4_ = rn_stages(st, lambda: mmb[st["m"]][:, 0:128], NKV, kAt[par], [f"kAt{par}"], idx)

                def s1():
                    r1()
                    ps = mmb[st["m"]]
                    S_.op("act", lambda e: e.activation(out=V_A[:, rb, :].rearrange("p (g c) -> p g c", c=65)[:, :, 0:64],
                                                        in_=ps[:, 128:256].rearrange("p (g d) -> p g d", g=NKV), func=AF.Copy),
                          reads=[f"mm{st['m']}", "V_A_ones"], writes=[f"V_A{rb}"])
                    S_.op("dve", lambda e: e.tensor_tensor(out=z, in0=ps[:, 256:264], in1=bfg[:], op=ALU.add), reads=[f"mm{st['m']}", "bfg"], writes=[fr + "z"])
                    S_.op("dve", lambda e: e.scalar_tensor_tensor(out=a_, in0=z, scalar=-1.0, in1=z, op0=ALU.mult, op1=ALU.max), reads=[fr + "z"], writes=[fr + "a"])
                    S_.op("dve", lambda e: e.tensor_single_scalar(out=mn, in_=z, scalar=0.0, op=ALU.min), reads=[fr + "z"], writes=[fr + "m"])

                def s2():
                    r2()
                    S_.op("act", lambda e: e.activation(out=a_, in_=a_, func=AF.Exp, scale=-1.0), reads=[fr + "a"], writes=[fr + "a"])
                    S_.op("act", lambda e: e.activation(out=a_, in_=a_, func=AF.Ln, bias=1.0), reads=[fr + "a"], writes=[fr + "a"])

                def s3():
                    r3()
                    S_.op("dve", lambda e: e.tensor_tensor(out=mn, in0=mn, in1=a_, op=ALU.subtract), reads=[fr + "m", fr + "a"], writes=[fr + "m"])

                def s4():
                    r4_()
                    fns = [lambda e: e.matmul(misc[:, 0:NHB], tri[:], mn, start=True, stop=(blk == 0))]
                    rds = ["tri", fr + "m"]
                    for j in range(i):
                        fbj = 64 + (j % 4) * 24
                        last = (j == i - 1) and (c == 0)
                        fns.append(lambda e, fbj=fbj, last=last: e.matmul(misc[:, 0:NHB], ones_f[:], small[:, fbj + 16:fbj + 24], start=False, stop=last))
                        rds += [f"fg{j % 4}m", "ones_f"]
                    if c > 0:
                        fns.append(lambda e: e.matmul(misc[:, 0:NHB], sel[:], cprev[:], start=False, stop=True))
                        rds += ["sel", "cprev"]
                    S_.group("pe", fns, reads=rds, writes=["misc"])
                    S_.op("dve", lambda e: e.tensor_copy(cchunk[:, i, :], misc[:, 0:NHB]), reads=["misc"], writes=[f"cc{i}"])
                    S_.op("dve", lambda e: e.tensor_scalar(out=negc[:, blk, :], in0=cchunk[:, i, :], scalar1=-1.0, scalar2=None, op0=ALU.mult),
                          reads=[f"cc{i}"], writes=[f"negc{blk}"])

                def s5():
                    t = next_tp()
                    st["t"] = t
                    tpv = tpb[t][:].rearrange("p (k n) -> p k n", k=KC)
                    S_.group("pe", [lambda e, g=g: e.transpose(tpv[0:66, g, :], kAt[par][:, g, :], ident_bf[:]) for g in range(NKV)],
                             reads=[f"kAt{par}", "ident_bf"], writes=[f"tp{t}"])

                def s6():
                    t = st["t"]
                    tpv = tpb[t][:].rearrange("p (k n) -> p k n", k=KC)
                    S_.op("act", lambda e: e.activation(out=KT_A[0:66, :, rb * 128:(rb + 1) * 128], in_=tpv[0:66, 0:NKV, :], func=AF.Copy),
                          reads=[f"tp{t}"], writes=[f"KT_A{rb}"])

                return [s0, s1, s2, s3, s4, s5, s6]

            def qk_item(slab_id, kind, i):
                blk = 4 * c + i
                idx = item_ctr[0]
                item_ctr[0] += 1
                par = idx % 2
                st = {}

                def s0():
                    if i == 0:
                        st_sl["sl"] = ws_get(slab_id)
                    sl = st_sl["sl"]
                    m = next_mm()
                    st["m"] = m
                    ps = mmb[m]
                    S_.group("pe", [lambda e, kc=kc: e.matmul(ps[:], xnT[:, kc, i * 128:(i + 1) * 128],
                                                               wslot[sl][:, kc * 512:(kc + 1) * 512], start=(kc == 0), stop=(kc == KC - 1))
                                    for kc in range(KC)], reads=[f"xnT{i}", f"wslot{sl}"], writes=[f"mm{m}"])
                    if i == 3:
                        ws_done()

                if kind == "vB":
                    def s1v():
                        ps = mmb[st["m"]]
                        S_.op("act", lambda e: e.activation(out=V_B[:, blk, :].rearrange("p (h c) -> p h c", c=65)[:, :, 0:64],
                                                            in_=ps[:].rearrange("p (h d) -> p h d", h=NHB), func=AF.Copy),
                              reads=[f"mm{st['m']}", "V_B_ones"], writes=[f"V_B{blk}"])
                    return [s0, s1v]
                tile_ = {"qA": qAt, "qB": qBt, "kB": kBt}[kind][par]
                tname = f"{kind}t{par}"
                nrow = 66 if kind == "qA" else 68
                s1, s2, s3, r4_ = rn_stages(st, lambda: mmb[st["m"]][:], 8, tile_, [tname], idx)

                def s4():
                    r4_()
                    if kind == "qB":
                        S_.op("dve", lambda e: e.tensor_copy(tile_[:, :, 64:65], cchunk[:, i, :].unsqueeze(2)), reads=[f"cc{i}"], writes=[tname])
                    if kind == "kB":
                        t1 = small[:, 160 + 16 * par:168 + 16 * par]
                        t2 = small[:, 168 + 16 * par:176 + 16 * par]
                        ng = negc[:, blk, :]
                        col = lambda k_: tile_[:, :, k_:k_ + 1].rearrange("p h o -> p (h o)")
                        S_.op("dve", lambda e: e.tensor_copy(col(65), ng), reads=[f"negc{blk}"], writes=[tname])
                        S_.op("dve", lambda e: e.tensor_tensor(out=t1, in0=ng, in1=col(65), op=ALU.subtract), reads=[f"negc{blk}", tname], writes=[f"spl{par}a"])
                        S_.op("dve", lambda e: e.tensor_copy(col(66), t1), reads=[f"spl{par}a"], writes=[tname])
                        S_.op("dve", lambda e: e.tensor_tensor(out=t2, in0=t1, in1=col(66), op=ALU.subtract), reads=[f"spl{par}a", tname], writes=[f"spl{par}b"])
                        S_.op("dve", lambda e: e.tensor_copy(col(67), t2), reads=[f"spl{par}b"], writes=[tname])

                def s5():
                    t = next_tp()
                    st["t"] = t
                    tpv = tpb[t][:].rearrange("p (k n) -> p k n", k=KC)
                    S_.group("pe", [lambda e, h=h: e.transpose(tpv[0:nrow, h, :], tile_[:, h, :], ident_bf[:]) for h in range(8)],
                             reads=[tname, "ident_bf"], writes=[f"tp{t}"])

                def s6():
                    t = st["t"]
                    tpv = tpb[t][:].rearrange("p (k n) -> p k n", k=KC)
                    if kind == "qA":
                        S_.op("act", lambda e: e.activation(out=QT_A[0:66, :, i, :].rearrange("p g (h n) -> p g h n", h=4),
                                                            in_=tpv[0:66, :, :].rearrange("p (g h) n -> p g h n", g=NKV),
                                                            func=AF.Copy, scale=scaleA[0:66, :]),
                              reads=[f"tp{t}", "scaleA"], writes=[f"QT_A{i}"])
                    elif kind == "qB":
                        S_.op("act", lambda e: e.activation(out=QT_B[0:68, :, i * 128:(i + 1) * 128], in_=tpv[0:68, :, :],
                                                            func=AF.Copy, scale=scaleB[0:68, :]),
                              reads=[f"tp{t}", "scaleB"], writes=[f"QT_B{i}"])
                    else:
                        S_.op("dve", lambda e: e.tensor_copy(KT_B[0:68, :, blk * 128:(blk + 1) * 128], tpv[0:68, :, :]),
                              reads=[f"tp{t}"], writes=[f"KT_B{blk}"])

                return [s0, s1, s2, s3, s4, s5, s6]

            st_sl = {}
            pipe += [i4_item(i) for i in range(4)]
            for slab_id, kind in ((SL_I[0], "qA"), (SL_I[2], "kB"), (SL_I[1], "qB"), (SL_I[3], "vB")):
                pipe += [qk_item(slab_id, kind, i) for i in range(4)]
            run_pipe(pipe, 7)
            S_.op("dve", lambda e: e.tensor_copy(cprev[:], cchunk[:, 3, :]), reads=["cc3"], writes=["cprev"])
            S_.dma("sp", d_g, gbuf[:], gM_d[:], writes=["gbuf"])

            items = []
            for h in range(NHB):
                NJ = 4 * c + 4
                for j in range(NJ):
                    items.append(dict(kind="B", h=h, j=j, first=(j == 0), last=(j == NJ - 1)))
            for i in range(4):
                blk = 4 * c + i
                for g in range(NKV):
                    pairs = ([(blk - 1, "P")] if blk > 0 else []) + [(blk, "C")]
                    for pi_, (kb, mk) in enumerate(pairs):
                        items.append(dict(kind="A", i=i, g=g, kb=kb, mk=mk, first=(pi_ == 0), last=(pi_ == len(pairs) - 1)))
            units = []
            for it in items:
                key = ("A", it["i"], it["g"]) if it["kind"] == "A" else ("B", it["h"])
                it["key"] = key
                if units and len(units[-1]) == 1 and units[-1][0]["key"] == key:
                    units[-1].append(it)
                else:
                    units.append([it])
            pend_A = []
            ag_i = [0]
            pend_fin = []
            oa_i = [0]
            un_i = [0]
            bc_i = [0]
            rd_i = [0]
            cur_oa = {}

            def acc(m):
                return (mmb[4], "mm4") if m == 4 else (tp1_f32, "tp1")

            def stage_S(unit):
                u = un_i[0] % 2
                u3 = un_i[0] % 3
                un_i[0] += 1
                cb = 0
                for e_, it in enumerate(unit):
                    if it["kind"] == "A":
                        N, off = 512, 0
                    else:
                        jj = it["j"] - 4 * c
                        off = max(jj, 0) * 128
                        N = 512 - off
                    if e_ == 1:
                        cb = unit[0]["N"] if unit[0]["N"] < 512 else 512
                    bk = cb // 512
                    m = 2 * u + bk
                    ps = pairb[u][:, cb:cb + N]
                    it["N"], it["off"] = N, off
                    it["ptap"] = PTp[u3][:, cb:cb + N]
                    it["ptres"] = R_PT[2 * u3 + bk]
                    if it["kind"] == "A":
                        i, g, kb = it["i"], it["g"], it["kb"]
                        rb = kb % 8
                        mask = maskP[:, g, :] if it["mk"] == "P" else maskC[:]
                        S_.group("pe", [lambda e: e.matmul(ps, KT_A[0:66, g, rb * 128:(rb + 1) * 128], QT_A[0:66, g, i, :], start=True, stop=False),
                                        lambda e: e.matmul(ps, ident_bf[:], mask, start=False, stop=True)],
                                 reads=[f"KT_A{rb}", f"QT_A{i}", "ident_bf", "maskP", "maskC"], writes=[f"mm{m}"])
                    else:
                        h, j = it["h"], it["j"]
                        fns = [lambda e: e.matmul(ps, KT_B[0:68, h, j * 128:(j + 1) * 128], QT_B[0:68, h, off:512], start=True, stop=(jj < 0),
                                                  skip_group_check=True)]
                        if jj >= 0:
                            fns.append(lambda e: e.matmul(ps[:, 0:128], ident_bf[:], maskC[:, 0:128], start=False, stop=True, skip_group_check=True))
                        S_.group("pe", fns, reads=[f"KT_B{j}"] + [f"QT_B{ii}" for ii in range(4)] + ["ident_bf", "maskC"], writes=[f"mm{m}"])
                W = cb + unit[-1]["N"]
                banks = sorted({2 * u + (0 if (k_ == 0) else (cb // 512)) for k_ in range(len(unit))})
                pts = sorted({2 * u3 + (0 if (k_ == 0) else (cb // 512)) for k_ in range(len(unit))})
                S_.op("act", lambda e: e.activation(out=PTp[u3][:, 0:W], in_=pairb[u][:, 0:W], func=AF.Exp),
                      reads=[f"mm{b_}" for b_ in banks], writes=[r_ for p_ in pts for r_ in R_PT[p_]])

            def stage_PV(it):
                key = it["key"]
                if it["first"]:
                    cur_oa[key] = (4, 5)[oa_i[0] % 2]
                    oa_i[0] += 1
                    for ent in [p for p in pend_fin if cur_oa[p[2]] == cur_oa[key] and p[2] != key]:
                        pend_fin.remove(ent)
                        stage_fin2(ent[1])
                m = cur_oa[key]
                oa, bank = acc(m)
                ptap, ptres = it["ptap"], it["ptres"]
                N, off = it["N"], it["off"]
                if it["kind"] == "A":
                    rb = it["kb"] % 8
                    g = it["g"]
                    S_.group("pe", [lambda e, hh=hh: e.matmul(oa[:, hh * 65:(hh + 1) * 65], ptap[:, hh * 128:(hh + 1) * 128],
                                                               V_A[:, rb, g * 65:(g + 1) * 65], start=(it["first"] and hh == 0), stop=it["last"],
                                                               skip_group_check=True)
                                    for hh in range(4)],
                             reads=[f"V_A{rb}", "V_A_ones"] + ptres, writes=[bank])
                else:
                    h, j = it["h"], it["j"]
                    S_.op("pe", lambda e: e.matmul(oa[0:65, off:512], V_B[:, j, h * 65:(h + 1) * 65], ptap, start=it["first"], stop=it["last"]),
                          reads=[f"V_B{j}", "V_B_ones"] + ptres, writes=[bank])

            RDP = [0, 32, 64]

            def stage_finA1(it):
                key = ("A", it["i"], it["g"])
                oa, bank = acc(cur_oa[key])
                g = it["g"]
                r = ag_i[0] % 2
                ag_i[0] += 1
                it["ag"] = r
                den4 = small[:, 40 + 8 * r:44 + 8 * r]
                oav = oa[:, 0:260].rearrange("p (h c) -> p h c", c=65)
                S_.op("dve", lambda e: e.tensor_tensor(out=den4, in0=oav[:, :, 64], in1=sinkb[:, g * 4:(g + 1) * 4], op=ALU.add),
                      reads=[bank, "sinkb"], writes=[f"den4{r}"])
                S_.op("dve", lambda e: e.reciprocal(den4, den4), reads=[f"den4{r}"], writes=[f"den4{r}"])
                S_.op("dve", lambda e: e.tensor_tensor(out=mtok[r][:].rearrange("p (h d) -> p h d", h=4), in0=oav[:, :, 0:64],
                                                       in1=den4.unsqueeze(2).to_broadcast([128, 4, 64]), op=ALU.mult),
                      reads=[bank, f"den4{r}"], writes=R_mtok[r])

            def stage_finA2(it):
                i, g = it["i"], it["g"]
                r = it["ag"]
                t = 0
                S_.group("pe", [lambda e, a=a: e.transpose(tpb[t][:, a * 128:(a + 1) * 128], mtok[r][:, a * 128:(a + 1) * 128], ident_bf[:])
                                for a in range(2)], reads=R_mtok[r] + ["ident_bf"], writes=[f"tp{t}"])
                S_.op("dve", lambda e: e.tensor_copy(mixT[:, g * 2:g * 2 + 2, i * 128:(i + 1) * 128],
                                                     tpb[t][:, 0:256].rearrange("p (a n) -> p a n", a=2)),
                      reads=[f"tp{t}"], writes=[f"mixT{g * 2}", f"mixT{g * 2 + 1}"])

            def stage_fin1(it):
                key = ("B", it["h"])
                oa, bank = acc(cur_oa[key])
                r = rd_i[0] % 3
                rd_i[0] += 1
                it["rd"] = r
                p0 = RDP[r]
                rrow = rden[p0:p0 + 1, :]
                if 4 * c + 4 <= 8 or it["h"] == NHB - 1:
                    S_.op("act", lambda e: e.activation(out=rrow, in_=oa[64:65, :], func=AF.Ln), reads=[bank], writes=[f"BC|rden{r}|sh4"])
                    S_.op("act", lambda e: e.activation(out=rrow, in_=rrow, func=AF.Exp, scale=-1.0), reads=[f"BC|rden{r}|sh4"], writes=[f"BC|rden{r}|sh4"])
                else:
                    S_.op("dve", lambda e: e.reciprocal(rrow, oa[64:65, :]), reads=[bank], writes=[f"BC|rden{r}|sh4"])

            def stage_fin2(it):
                key = ("B", it["h"])
                oa, bank = acc(cur_oa[key])
                b = 0
                r = it["rd"]
                p0 = RDP[r]
                S_.op("pe", lambda e: e.matmul(misc[0:64, :], ones_f[p0:p0 + 1, 0:64], rden[p0:p0 + 1, :], start=True, stop=True),
                      reads=["ones_f", f"BC|rden{r}|sh4"], writes=["misc"])
                S_.op("dve", lambda e: e.tensor_copy(bc_sb[b][0:64, :], misc[0:64, :]), reads=["misc"], writes=R_bc[b])
                h = it["h"]
                par = h % 2
                ec = 4 + h // 2
                S_.op("dve", lambda e: e.tensor_tensor(out=mixT[par * 64:(par + 1) * 64, ec, :], in0=oa[0:64, :], in1=bc_sb[b][0:64, :], op=ALU.mult),
                      reads=[bank] + R_bc[b], writes=[f"mixT{ec}"])

            FDELAY = 4
            n_u = len(units)
            k = 0
            while k < n_u + 2 or pend_fin or pend_A:
                if k < n_u:
                    stage_S(units[k])
                if 0 <= k - 2 < n_u:
                    for it in units[k - 2]:
                        stage_PV(it)
                        if it["last"]:
                            if it["kind"] == "A":
                                while len(pend_A) > 1:
                                    stage_finA2(pend_A.pop(0)[1])
                                stage_finA1(it)
                                pend_A.append((k + 2, it))
                            else:
                                while len(pend_fin) > 2:
                                    stage_fin2(pend_fin.pop(0)[1])
                                stage_fin1(it)
                                pend_fin.append((k + FDELAY, it, it["key"]))
                while pend_fin and pend_fin[0][0] <= k:
                    stage_fin2(pend_fin.pop(0)[1])
                while pend_A and pend_A[0][0] <= k:
                    stage_finA2(pend_A.pop(0)[1])
                k += 1

            def op_item(dh, i):
                st = {}

                def s0():
                    if i == 0 and dh == 0:
                        st_sl["o"] = [ws_get(SL_O[0]), ws_get(SL_O[1], ahead=1)]
                    sl = st_sl["o"][dh]
                    m = next_mm()
                    st["m"] = m
                    ps = mmb[m]
                    S_.group("pe", [lambda e, ec=ec: e.matmul(ps[:], mixT[:, ec, i * 128:(i + 1) * 128], wslot[sl][:, ec * 512:(ec + 1) * 512],
                                                               start=(ec == 0), stop=(ec == KC - 1)) for ec in range(KC)],
                             reads=[f"mixT{ec}" for ec in range(KC)] + [f"wslot{sl}"], writes=[f"mm{m}"])
                    if i == 3 and dh == 1:
                        ws_done()
                        ws_done()

                def s1():
                    ps = mmb[st["m"]]
                    S_.op("dve", lambda e: e.tensor_tensor(out=xh[:, i, dh * 512:(dh + 1) * 512], in0=ps[:], in1=xh[:, i, dh * 512:(dh + 1) * 512], op=ALU.add),
                          reads=[f"mm{st['m']}", f"xh{i}"], writes=[f"xh{i}"])

                return [s0, s1]

            o = {(i, dh): op_item(dh, i) for i in range(4) for dh in range(2)}
            n_ = [norm_item(i, "gbuf", evac="act") for i in range(4)]
            pipe = [o[0, 0], o[0, 1], o[1, 0], n_[0], o[1, 1], o[2, 0], n_[1], o[2, 1], o[3, 0], n_[2], o[3, 1], None, n_[3]]
            run_pipe(pipe, 6)
            last_chunk = (q == NSEQ - 1 and c == NCH - 1)
            if not last_chunk:
                S_.dma("sp", d_g, gbuf[:], gA_d[:], writes=["gbuf"])

            def ffn_up(g):
                sl = ws_get(SL_U[g])
                hb = g % 2
                ms = [next_mm() for _ in range(4)]
                parts = [(0, 384, [0, 1, 2]), (384, 512, [3])] if g == 0 else [(0, 512, [0, 1, 2, 3])]
                for c0, c1, tiles in parts:
                    for fc in range(4):
                        ps = mmb[ms[fc]]
                        S_.group("pe", [lambda e, kc=kc: e.matmul(ps[:, c0:c1], wslot[sl][:, kc * 512 + fc * 128:kc * 512 + (fc + 1) * 128],
                                                                   xnT[:, kc, c0:c1], start=(kc == 0), stop=(kc == KC - 1)) for kc in range(KC)],
                                 reads=[f"xnT{i}" for i in tiles] + [f"wslot{sl}"], writes=[f"mm{ms[fc]}"])
                for fc in range(4):
                    m = ms[fc]
                    ps = mmb[m]
                    rb_ = fc % 2
                    S_.op("act", lambda e: e.activation(out=rtmp[rb_][:], in_=ps[:], func=AF.Relu), reads=[f"mm{m}"], writes=R_rtmp[rb_])
                    S_.op("pool", lambda e: e.tensor_tensor(out=hid[hb][:, fc, :], in0=rtmp[rb_][:], in1=rtmp[rb_][:], op=ALU.mult),
                          reads=R_rtmp[rb_], writes=R_hid[hb])
                ws_done()

            def ffn_down(g):
                sl = ws_get(SL_D[g])
                hb = g % 2
                nq, ncn = (q, c + 1) if c + 1 < NCH else (q + 1, 0)
                for i in range(4):
                    for dh in range(2):
                        m = next_mm()
                        ps = mmb[m]
                        S_.group("pe", [lambda e, fc=fc: e.matmul(ps[:], hid[hb][:, fc, i * 128:(i + 1) * 128],
                                                                   wslot[sl][:, fc * 1024 + dh * 512:fc * 1024 + (dh + 1) * 512],
                                                                   start=(fc == 0), stop=(fc == 3)) for fc in range(4)],
                                 reads=R_hid[hb] + [f"wslot{sl}"], writes=[f"mm{m}"])
                        if g < NG - 1:
                            S_.op("dve", lambda e: e.tensor_tensor(out=xh[:, i, dh * 512:(dh + 1) * 512], in0=ps[:], in1=xh[:, i, dh * 512:(dh + 1) * 512], op=ALU.add),
                                  reads=[f"mm{m}", f"xh{i}"], writes=[f"xh{i}"])
                        else:
                            yb = (2 * i + dh) % 4
                            r0 = q * S + (4 * c + i) * 128
                            S_.op("dve", lambda e: e.tensor_tensor(out=yout[yb][:], in0=ps[:], in1=xh[:, i, dh * 512:(dh + 1) * 512], op=ALU.add),
                                  reads=[f"mm{m}", f"xh{i}"], writes=R_yout[yb])
                            S_.dma("sp", d_yb[yb], y[r0:r0 + 128, dh * 512:(dh + 1) * 512], yout[yb][:], reads=R_yout[yb], writes=[f"y{yb}"])
                    if g == NG - 1 and nq < NSEQ:
                        r1 = nq * S + (4 * ncn + i) * 128
                        S_.dma("sp", d_x[i], xh[:, i, :], x[r1:r1 + 128, :], writes=[f"xh{i}"])
                ws_done()

            for g in range(NG + 1):
                if g < NG:
                    ffn_up(g)
                if g >= 1:
                    ffn_down(g - 1)

    S_.wait_all("sp")
    return nc, S_


_PROG_CACHE = {}


def make_in_maps(x2d_list, w_in, w_out, w_up, w_down, attn_norm_g, mlp_norm_g, b_forget, q_norm_a, k_norm_a,
                 sink_logits, q_norm_b, k_norm_b):
    c = host_consts()
    f = lambda a: np.ascontiguousarray(np.asarray(a, dtype=np.float32))
    common = {
        "w_in": f(w_in), "w_out": f(w_out), "w_up": f(w_up), "w_down": f(w_down),
        "gA": f(np.broadcast_to(np.asarray(attn_norm_g)[None, :], (128, D))),
        "gM": f(np.broadcast_to(np.asarray(mlp_norm_g)[None, :], (128, D))),
        "bfg": f(np.broadcast_to(np.asarray(b_forget)[None, :], (128, NHB))),
        "gains": f(np.stack([np.asarray(q_norm_a), np.asarray(k_norm_a), np.asarray(q_norm_b), np.asarray(k_norm_b)], axis=1)),
        "sinks": f(np.broadcast_to(np.asarray(sink_logits)[None, :], (128, NHA))),
        **{k: f(v) for k, v in c.items()},
    }
    return [{"x": f(xs), **common} for xs in x2d_list]


def kernel(x, attn_norm_g, w_in, b_forget, q_norm_a, k_norm_a, sink_logits, q_norm_b, k_norm_b, w_out,
           mlp_norm_g, w_up, w_down):
    x = np.asarray(x)
    B, S, _ = x.shape
    DFF = np.asarray(w_up).shape[1]
    n = 8
    per = B // n
    key = (per, S, DFF)
    if key not in _PROG_CACHE:
        _PROG_CACHE[key] = build_program(per, S, DFF)[0]
    nc = _PROG_CACHE[key]
    shards = [x[i * per:(i + 1) * per].reshape(per * S, D) for i in range(n)]
    in_maps = make_in_maps(shards, w_in, w_out, w_up, w_down, attn_norm_g, mlp_norm_g, b_forget, q_norm_a, k_norm_a,
                           sink_logits, q_norm_b, k_norm_b)
    res = run_bass_kernel_spmd(nc, in_maps, core_ids=list(range(n)))
    out = np.concatenate([np.asarray(r["y"]).reshape(per, S, D) for r in res.results], axis=0)
    return out.astype(np.float32)
```
